# Optimizing a Trainium2 kernel written in Bass

```python
import jax, jax.numpy as jnp
from jax import lax
import numpy as np

D_MODEL = 1024
BATCH = 8
SEQ = 2048
DEPTH = 4

N_MIXERS = 2
N_CONV_LAYERS = (DEPTH + 1) // 2
N_SG_LAYERS = DEPTH // 2
CONV_WIDTH = 3
CONV_GROUPS = 16
SG_WIDTH = D_MODEL
SG_GROUPS = 8
SG_GROUP_DIM = SG_WIDTH // SG_GROUPS
CHUNK = 128
D_FF = int(-(-(8 * D_MODEL // 3) // 256) * 256) if (8 * D_MODEL) % 3 == 0 else ((8 * D_MODEL // 3) // 256 + 1) * 256
PLE_DIM = 256
RMS_EPS = 1e-6
LN_EPS = 1e-5

kernel_name = "hybrid_shortconv_gmlp_trunk"


def rms_norm(x, g):
    xf = x.astype(jnp.float32)
    var = jnp.mean(xf * xf, axis=-1, keepdims=True)
    return (xf * lax.rsqrt(var + RMS_EPS)).astype(x.dtype) * g


def layer_norm(x, g, b):
    xf = x.astype(jnp.float32)
    mu = jnp.mean(xf, axis=-1, keepdims=True)
    xc = xf - mu
    var = jnp.mean(xc * xc, axis=-1, keepdims=True)
    return (xc * lax.rsqrt(var + LN_EPS)).astype(x.dtype) * g + b


def causal_depthwise_conv(z, w_conv):
    return lax.conv_general_dilated(
        z, w_conv[:, None, :], window_strides=(1,), padding=[(CONV_WIDTH - 1, 0)],
        dimension_numbers=("NWC", "WIO", "NWC"), feature_group_count=z.shape[-1])


def short_conv_mixer(h, w_in, w_conv, w_out):
    bcx = h @ w_in
    b_gate, c_gate, xx = jnp.split(bcx, 3, axis=-1)
    y = causal_depthwise_conv(c_gate * xx, w_conv)
    return (b_gate * y) @ w_out


def spatial_gating_mixer(h, w_in, v_gain, v_bias, w_s, b_s, w_out):
    bsz, s, _ = h.shape
    u, v = jnp.split(h @ w_in, 2, axis=-1)
    v = layer_norm(v, v_gain, v_bias)
    n_chunks = s // CHUNK
    v = v.reshape(bsz, n_chunks, CHUNK, SG_GROUPS, SG_GROUP_DIM)
    causal = jnp.tril(jnp.ones((CHUNK, CHUNK), dtype=bool))
    w_masked = jnp.where(causal[None], w_s, jnp.zeros_like(w_s))
    mixed = jnp.einsum("gts,bcsgd->bctgd", w_masked, v) + b_s.T[:, :, None]
    y = u * mixed.reshape(bsz, s, SG_WIDTH)
    return y @ w_out


def swiglu_ffn(h, w_gate, w_up, w_down):
    return (jax.nn.silu(h @ w_gate) * (h @ w_up)) @ w_down


def setup_inputs(seed: int = 0) -> dict:
    key = jax.random.key(seed)
    ks = jax.random.split(key, 24)
    f32 = jnp.float32
    D = D_MODEL

    def nrm(k, shape, scale):
        return jax.random.normal(k, shape, f32) * scale

    def gain(k, shape):
        return 1.0 + 0.02 * jax.random.normal(k, shape, f32)

    causal = np.tril(np.ones((CHUNK, CHUNK), dtype=np.float32))
    return {
        "x": nrm(ks[0], (BATCH, SEQ, D), 1.0),
        "p": nrm(ks[1], (DEPTH, BATCH, SEQ, PLE_DIM), 1.0),
        "mix_norm": gain(ks[2], (DEPTH, D)),
        "conv_w_in": nrm(ks[3], (N_CONV_LAYERS, D, 3 * D), D ** -0.5),
        "conv_w": nrm(ks[4], (N_CONV_LAYERS, CONV_WIDTH, D), CONV_WIDTH ** -0.5),
        "conv_w_out": nrm(ks[5], (N_CONV_LAYERS, D, D), D ** -0.5),
        "sg_w_in": nrm(ks[6], (N_SG_LAYERS, D, 2 * SG_WIDTH), D ** -0.5),
        "sg_v_gain": gain(ks[7], (N_SG_LAYERS, SG_WIDTH)),
        "sg_v_bias": nrm(ks[8], (N_SG_LAYERS, SG_WIDTH), 0.02),
        "sg_w_spatial": nrm(ks[9], (N_SG_LAYERS, SG_GROUPS, CHUNK, CHUNK), 0.5 * CHUNK ** -0.5) * causal,
        "sg_b_spatial": gain(ks[10], (N_SG_LAYERS, SG_GROUPS, CHUNK)),
        "sg_w_out": nrm(ks[11], (N_SG_LAYERS, SG_WIDTH, D), SG_WIDTH ** -0.5),
        "ffn_norm": gain(ks[12], (DEPTH, D)),
        "ffn_w_gate": nrm(ks[13], (DEPTH, D, D_FF), D ** -0.5),
        "ffn_w_up": nrm(ks[14], (DEPTH, D, D_FF), D ** -0.5),
        "ffn_w_down": nrm(ks[15], (DEPTH, D_FF, D), D_FF ** -0.5),
        "ple_norm": gain(ks[16], (DEPTH, D)),
        "ple_w_gate": nrm(ks[17], (DEPTH, D, D), D ** -0.5),
        "ple_w_proj": nrm(ks[18], (DEPTH, PLE_DIM, D), 0.5 * PLE_DIM ** -0.5),
        "final_norm": gain(ks[19], (D,)),
    }


def reference(x, p, mix_norm, conv_w_in, conv_w, conv_w_out, sg_w_in, sg_v_gain, sg_v_bias,
              sg_w_spatial, sg_b_spatial, sg_w_out, ffn_norm, ffn_w_gate, ffn_w_up, ffn_w_down,
              ple_norm, ple_w_gate, ple_w_proj, final_norm):
    h = x
    for i in range(DEPTH):
        j = i // N_MIXERS
        hn = rms_norm(h, mix_norm[i])
        if i % N_MIXERS == 0:
            mix = short_conv_mixer(hn, conv_w_in[j], conv_w[j], conv_w_out[j])
        else:
            mix = spatial_gating_mixer(hn, sg_w_in[j], sg_v_gain[j], sg_v_bias[j],
                                       sg_w_spatial[j], sg_b_spatial[j], sg_w_out[j])
        h = h + mix
        h = h + swiglu_ffn(rms_norm(h, ffn_norm[i]), ffn_w_gate[i], ffn_w_up[i], ffn_w_down[i])
        gate = jax.nn.sigmoid(rms_norm(h, ple_norm[i]) @ ple_w_gate[i])
        h = h + gate * (p[i] @ ple_w_proj[i])
    return rms_norm(h, final_norm)
```

```python
from contextlib import ExitStack

import numpy as np
import concourse.bass as bass
import concourse.mybir as mybir
from concourse.bass_utils import run_bass_kernel_spmd

F32 = mybir.dt.float32
BF16 = mybir.dt.bfloat16
ALU = mybir.AluOpType
AF = mybir.ActivationFunctionType

D = 1024
KC = 8
DFF = 2816
FC = 22
PLE = 256
DEPTH = 4
RMS_EPS = 1e-6
LN_EPS = 1e-5
N_CORES = 8
SEQ = 2048

C_MIX = 0
C_FFN = 32
C_PLE = 64
C_FIN = 96
C_CONVW = 104
C_VBIAS = 152
C_VGAIN = 168
NS = 184

ENGS = ("pe", "act", "dve", "pool", "sp")
PROG_STATS = {}


class Op:
    __slots__ = ("eng", "emit", "deps", "signal", "seq", "dma_sem", "dma_val", "idx", "eidx")

    def __init__(self, eng, emit):
        self.eng = eng
        self.emit = emit
        self.deps = []
        self.signal = False
        self.seq = 0
        self.dma_sem = None
        self.dma_val = 0
        self.idx = 0
        self.eidx = 0


class Prog:
    NEAR = 10 ** 9

    def __init__(self, nc):
        self.nc = nc
        self.ops = []
        self.last_w = {}
        self.readers = {}
        self.eng_count = {e: 0 for e in ENGS}
        self.dma_sems = {}
        self._sem_ctx = []

    def _new_sem(self, name):
        cm = self.nc.semaphore(name)
        s = cm.__enter__()
        self._sem_ctx.append(cm)
        return s

    def add(self, eng, emit, reads=(), writes=(), dma_key=None):
        op = Op(eng, emit)
        op.idx = len(self.ops)
        op.eidx = self.eng_count[eng]
        self.eng_count[eng] += 1
        deps = {}

        def dep(o):
            if o is None or o is op:
                return
            deps[o.idx] = o

        for r in reads:
            dep(self.last_w.get(r))
        for w in writes:
            dep(self.last_w.get(w))
            for rd in self.readers.get(w, ()):
                dep(rd)
        for w in writes:
            self.last_w[w] = op
            self.readers[w] = []
        for r in reads:
            if r in writes:
                continue
            lst = self.readers.setdefault(r, [])
            if dma_key is None:
                lst[:] = [o for o in lst if not (o.eng == eng and o.dma_sem is None)]
            lst.append(op)
        if dma_key is not None:
            ent = self.dma_sems.get(dma_key)
            if ent is None:
                ent = [self._new_sem("dq_%s" % dma_key), 0]
                self.dma_sems[dma_key] = ent
            ent[1] += 16
            op.dma_sem = ent[0]
            op.dma_val = ent[1]
        op.deps = list(deps.values())
        self.ops.append(op)
        return op

    def _skip_same_engine(self, op, d):
        if d.eng != op.eng or d.dma_sem is not None:
            return False
        if d.eng == "pe":
            return True
        return op.eidx - d.eidx > self.NEAR

    def finalize(self, block):
        for op in self.ops:
            for d in op.deps:
                if d.dma_sem is None and not self._skip_same_engine(op, d):
                    d.signal = True
        esem = {e: self._new_sem("es_" + e) for e in ENGS}
        cnt = {e: 0 for e in ENGS}
        for op in self.ops:
            if op.dma_sem is None and op.signal:
                cnt[op.eng] += 1
                op.seq = cnt[op.eng]
        self.sig_counts = dict(cnt)
        per_eng = {e: [o for o in self.ops if o.eng == e] for e in ENGS}

        def run(engname, eh):
            known = {}
            for op in per_eng[engname]:
                need = {}
                for d in op.deps:
                    if d.dma_sem is not None:
                        key = ("d", id(d.dma_sem))
                        sem, val = d.dma_sem, d.dma_val
                    else:
                        if self._skip_same_engine(op, d):
                            continue
                        key = ("e", d.eng)
                        sem, val = esem[d.eng], d.seq
                    if known.get(key, 0) >= val:
                        continue
                    if key not in need or need[key][1] < val:
                        need[key] = (sem, val)
                for key, (sem, val) in need.items():
                    eh.wait_ge(sem, val)
                    known[key] = val
                inst = op.emit(eh)
                if op.dma_sem is not None:
                    inst.then_inc(op.dma_sem, 16)
                elif op.signal:
                    inst.then_inc(esem[engname], 1)

        @block.tensor
        def _(e):
            run("pe", e)

        @block.scalar
        def _(e):
            run("act", e)

        @block.vector
        def _(e):
            run("dve", e)

        @block.gpsimd
        def _(e):
            run("pool", e)

        @block.sync
        def _(e):
            run("sp", e)

    def close(self):
        for cm in reversed(self._sem_ctx):
            cm.__exit__(None, None, None)
        self._sem_ctx = []


def build_program(T=SEQ, layers=(0, 1, 2, 3), final=True):
    TT = min(512, T)
    NT = T // TT
    NCH = T // 128
    nc = bass.Bass("TRN2", target_bir_lowering=False)

    def din(name, shape):
        return nc.dram_tensor(name, list(shape), F32, kind="ExternalInput").ap()

    xT = din("xT", [D, T])
    pT = din("pT", [DEPTH, PLE, T])
    smalls_d = din("smalls", [128, NS])
    conv_w_in = din("conv_w_in", [2, D, 3 * D])
    conv_w_out = din("conv_w_out", [2, D, D])
    sg_w_in = din("sg_w_in", [2, D, 2 * D])
    sg_w_out = din("sg_w_out", [2, D, D])
    sg_bs = din("sg_bs", [2, D])
    sg_wsT = din("sg_wsT", [2, 128, D])
    ffn_w_gate = din("ffn_w_gate", [DEPTH, D, DFF])
    ffn_w_up = din("ffn_w_up", [DEPTH, D, DFF])
    ffn_w_down = din("ffn_w_down", [DEPTH, DFF, D])
    ple_w_gate = din("ple_w_gate", [DEPTH, D, D])
    ple_w_proj = din("ple_w_proj", [DEPTH, PLE, D])
    outT = nc.dram_tensor("outT", [D, T], F32, kind="ExternalOutput").ap()

    NSLOT = 10
    NSTG = 3
    SLOT_EL = 2048
    STG_EL = 1024

    with ExitStack() as es:
        def sb(name, shape, dt):
            return es.enter_context(nc.sbuf_tensor(name, list(shape), dt))

        h = sb("h", [128, KC, T], F32)
        hn = sb("hn", [128, KC, T], BF16)
        act = sb("act", [128, KC, T], BF16)
        slots = [sb("slot%d" % i, [128, SLOT_EL], BF16) for i in range(NSLOT)]
        stgs = [sb("stg%d" % i, [128, STG_EL], F32) for i in range(NSTG)]
        smalls = sb("smalls_sb", [128, NS], F32)
        onesD = sb("onesD", [128, 128], BF16)
        ones1 = sb("ones1", [128, 128], BF16)
        rstd_t = [sb("rstd%d" % i, [128, TT], F32) for i in range(1)]
        NSQ = 8
        tmpA = [sb("tmpA%d" % i, [128, TT], F32) for i in range(2)]
        tmpB = [sb("tmpB%d" % i, [128, TT], F32) for i in range(2)]
        zbuf = sb("zbuf", [128, 2 + 2048], F32)
        vhat_all = zbuf[:, 0:D].bitcast(BF16)
        vhat_b = [vhat_all[:, 0:D], vhat_all[:, D:2 * D]]
        sq_all = zbuf[:, 0:2048].bitcast(BF16)
        sq_t = [sq_all[:, i * 512:i * 512 + TT] for i in range(NSQ)]
        SCR_KEYS = (["zpad", ("vhat", 0), ("vhat", 1)] + [("z", tt) for tt in range(NT)] +
                    [("sq", i) for i in range(NSQ)])
        ACT_KEYS = [("act", j, tt) for j in range(KC) for tt in range(NT)]
        Rt = sb("Rt", [128, D], F32)
        wsT = sb("wsT", [128, D], BF16)
        stats = sb("stats", [128, 2, 2, 6], F32)
        mv = sb("mv", [128, 2, 2], F32)
        lnr = sb("lnr", [128, 2, 4], F32)
        epsr = sb("epsr", [128, 2], F32)
        ps = es.enter_context(nc.psum_tensor("ps", [128, 8, 512], F32))

        P = Prog(nc)
        blocks = []

        def add_block(loads, compute, hold=0):
            blocks.append({"loads": loads, "compute": compute, "hold": hold})

        def tl(tt):
            return slice(tt * TT, (tt + 1) * TT)

        P.add("pool", lambda e: e.memset(onesD[:], 1.0 / D), writes=["onesD"])
        P.add("pool", lambda e: e.memset(ones1[:], 1.0), writes=["ones1"])
        P.add("pool", lambda e: e.memset(epsr[:, 0:1], RMS_EPS), writes=["eps"])
        P.add("pool", lambda e: e.memset(epsr[:, 1:2], LN_EPS), writes=["eps"])
        P.add("sp", lambda e: e.dma_start(out=smalls[:], in_=smalls_d), writes=["smalls"], dma_key="sm")
        xv = xT.rearrange("(kc p) t -> p kc t", p=128)
        for tt in range(NT):
            P.add("sp", lambda e, tt=tt: e.dma_start(out=h[:, :, tl(tt)], in_=xv[:, :, tl(tt)]),
                  writes=[("h", kc, tt) for kc in range(KC)], dma_key="x%d" % tt)

        state = {"slot": 0, "stg": 0, "cast": 0}
        cast_engs = ("pool",)

        def load_to(src_ap, shape, dst2d, wkeys):
            a, b = shape
            assert b <= STG_EL
            rows = max(1, STG_EL // b)
            a0 = 0
            while a0 < a:
                a1 = min(a, a0 + rows)
                m = (a1 - a0) * b
                gi = state["stg"] % NSTG
                state["stg"] += 1
                stg_v = stgs[gi][:, 0:m].rearrange("p (a b) -> p a b", a=a1 - a0)
                P.add("sp", lambda e, stg_v=stg_v, a0=a0, a1=a1: e.dma_start(out=stg_v, in_=src_ap[:, a0:a1, :]),
                      writes=[("stg", gi)], dma_key="stg%d" % gi)
                dst = dst2d[:, a0 * b:a0 * b + m]
                srcv = stgs[gi][:, 0:m]
                P.add("pool", lambda e, dst=dst, srcv=srcv: e.tensor_copy(out=dst, in_=srcv),
                      reads=[("stg", gi)], writes=wkeys)
                a0 = a1

        def load_block(src_ap, shape, si):
            a, b = shape
            n = a * b
            assert n <= SLOT_EL and b <= STG_EL
            rows = max(1, STG_EL // b)
            a0 = 0
            while a0 < a:
                a1 = min(a, a0 + rows)
                m = (a1 - a0) * b
                gi = state["stg"] % NSTG
                state["stg"] += 1
                ce = cast_engs[state["cast"] % len(cast_engs)]
                state["cast"] += 1
                stg_v = stgs[gi][:, 0:m].rearrange("p (a b) -> p a b", a=a1 - a0)
                P.add("sp", lambda e, stg_v=stg_v, a0=a0, a1=a1: e.dma_start(out=stg_v, in_=src_ap[:, a0:a1, :]),
                      writes=[("stg", gi)], dma_key="stg%d" % gi)
                dst = slots[si][:, a0 * b:a0 * b + m]
                srcv = stgs[gi][:, 0:m]
                if ce == "act":
                    P.add("act", lambda e, dst=dst, srcv=srcv: e.activation(out=dst, in_=srcv, func=AF.Copy),
                          reads=[("stg", gi)], writes=[("slot", si)])
                else:
                    P.add(ce, lambda e, dst=dst, srcv=srcv: e.tensor_copy(out=dst, in_=srcv),
                          reads=[("stg", gi)], writes=[("slot", si)])
                a0 = a1
            slot_v = slots[si][:, 0:n].rearrange("p (a b) -> p a b", a=a)
            return slot_v, ("slot", si)

        def wview(w2d):
            return w2d.rearrange("(kc p) n -> p kc n", p=128)

        cnt = {"sq": 0, "rstd": 0, "hn": 0, "tmp": 0, "g": 0, "out": 0}
        otiles = []
        per = (T * 2) // (4 * TT)
        for jj in range(6):
            a32 = act[:, jj, :].bitcast(F32)
            for hh in range(per):
                tts = sorted(set(((hh * TT * 2 + o) // TT) for o in range(0, 2 * TT, TT)) & set(range(NT)))
                otiles.append((a32[:, hh * TT:(hh + 1) * TT], [("act", jj, t_) for t_ in tts]))

        class Norm:
            def __init__(self, gcol0, to_out=False):
                self.gcol0 = gcol0
                self.to_out = to_out
                self.ri = {}

            def square_one(self, tt, kc):
                wr = [("sq", kc)] + (SCR_KEYS if kc == 0 else [])
                P.add("act", lambda e: e.activation(out=sq_t[kc], in_=h[:, kc, tl(tt)], func=AF.Square),
                      reads=[("h", kc, tt)], writes=wr)

            def squares(self, tt):
                for kc in range(KC):
                    self.square_one(tt, kc)

            def stats(self, tt):
                for kc in range(KC):
                    P.add("pe", lambda e, kc=kc: e.matmul(
                        ps[:, 7, 0:TT], onesD[:], sq_t[kc], start=(kc == 0), stop=(kc == KC - 1)),
                        reads=[("sq", kc), "onesD"], writes=[("ps", 7)])
                ri = 0
                cnt["rstd"] += 1
                self.ri[tt] = ri
                P.add("act", lambda e, ri=ri: e.activation(
                    out=rstd_t[ri][:], in_=ps[:, 7, 0:TT], func=AF.Ln, bias=epsr[:, 0:1]),
                    reads=[("ps", 7), "eps"], writes=[("rstd", ri)])
                P.add("act", lambda e, ri=ri: e.activation(
                    out=ps[:, 6, 0:TT], in_=rstd_t[ri][:], func=AF.Exp, scale=-0.5),
                    reads=[("rstd", ri)], writes=[("ps", 6)])

            def apply(self, tt, kcs):
                ri = self.ri[tt]
                gcol0 = self.gcol0
                for kc in kcs:
                    if not self.to_out:
                        P.add("dve", lambda e, kc=kc, tt=tt, ri=ri: e.scalar_tensor_tensor(
                            out=hn[:, kc, tl(tt)], in0=h[:, kc, tl(tt)],
                            scalar=smalls[:, gcol0 + kc:gcol0 + kc + 1], in1=ps[:, 6, 0:TT],
                            op0=ALU.mult, op1=ALU.mult),
                            reads=[("h", kc, tt), ("ps", 6), "smalls"], writes=[("hn", kc, tt)])
                    else:
                        oi = cnt["out"] % len(otiles)
                        cnt["out"] += 1
                        oap, okeys = otiles[oi]
                        P.add("dve", lambda e, kc=kc, tt=tt, oap=oap: e.scalar_tensor_tensor(
                            out=oap, in0=h[:, kc, tl(tt)],
                            scalar=smalls[:, gcol0 + kc:gcol0 + kc + 1], in1=ps[:, 6, 0:TT],
                            op0=ALU.mult, op1=ALU.mult),
                            reads=[("h", kc, tt), ("ps", 6), "smalls"], writes=[("otile", oi)] + okeys)
                        P.add("sp", lambda e, kc=kc, tt=tt, oap=oap: e.dma_start(
                            out=outT[kc * 128:(kc + 1) * 128, tl(tt)], in_=oap),
                            reads=[("otile", oi)], writes=[("out", kc, tt)], dma_key="out%d" % oi)

            def whole_tile(self, tt):
                self.squares(tt)
                self.stats(tt)
                self.apply(tt, range(KC))

        def norm_block(norm):
            def compute(views):
                for tt in range(NT):
                    norm.whole_tile(tt)
            add_block([], compute)

        APPLY_AT = {3: (0,), 4: (1, 2), 5: (3, 4), 6: (5, 6), 7: (7,)}
        SQ_LAG = 1
        STATS_AT = 1
        assert STATS_AT >= SQ_LAG

        def interleave(after, tt, dc):
            if after is None:
                return
            n = tt * KC + dc
            if tt >= 1 and dc == STATS_AT:
                after.stats(tt - 1)
            m = n - SQ_LAG
            if m >= 0:
                after.square_one(m // KC, m % KC)
            if tt >= 1 and dc in APPLY_AT:
                after.apply(tt - 1, APPLY_AT[dc])
            if dc == KC - 1 and tt == NT - 1:
                for m in range(n - SQ_LAG + 1, n + 1):
                    after.square_one(m // KC, m % KC)
                after.stats(tt)
                after.apply(tt, range(KC))

        def proj_residual(w2d, nk, after=None):
            wv = wview(w2d)

            def compute(views):
                for tt in range(NT):
                    for dc in range(KC):
                        blk, bkey = views[dc // 2]
                        di = dc % 2
                        b = 4 + (cnt["g"] % 2)
                        cnt["g"] += 1

                        def mm(e, blk=blk, di=di, tt=tt, b=b):
                            for j in range(nk):
                                i = e.matmul(ps[:, b, 0:TT], blk[:, j, di * 128:(di + 1) * 128],
                                             act[:, j, tl(tt)], start=(j == 0), stop=(j == nk - 1))
                            return i
                        P.add("pe", mm, reads=[bkey] + [("act", j, tt) for j in range(nk)],
                              writes=[("ps", b)])
                        P.add("dve", lambda e, dc=dc, tt=tt, b=b: e.tensor_tensor(
                            out=h[:, dc, tl(tt)], in0=ps[:, b, 0:TT], in1=h[:, dc, tl(tt)], op=ALU.add),
                            reads=[("ps", b), ("h", dc, tt)], writes=[("h", dc, tt)])
                        interleave(after, tt, dc)
            add_block([(wv[:, :, dp * 256:(dp + 1) * 256], (nk, 256)) for dp in range(4)], compute)

        def conv_mixer(j, after):
            wv = wview(conv_w_in[j])
            wc0 = C_CONVW + j * 24

            def setup(views):
                P.add("pool", lambda e: e.memset(zbuf[:, 0:2], 0.0), writes=SCR_KEYS)
            add_block([], setup)
            for cp in range(4):
                def compute(views, cp=cp):
                    (blkC, kC_), (blkX, kX_), (blkB, kB_) = views
                    for ci in range(2):
                        cc = cp * 2 + ci
                        for tt in range(NT):
                            g = cnt["g"] % 2
                            cnt["g"] += 1
                            bC, bX, bB = 3 * g, 3 * g + 1, 3 * g + 2

                            def mm(e, blk, b, ci=ci, tt=tt):
                                for kc in range(KC):
                                    i = e.matmul(ps[:, b, 0:TT], blk[:, kc, ci * 128:(ci + 1) * 128],
                                                 hn[:, kc, tl(tt)], start=(kc == 0), stop=(kc == KC - 1))
                                return i
                            hreads = [("hn", kc, tt) for kc in range(KC)]
                            P.add("pe", lambda e, blk=blkC, b=bC, mm=mm: mm(e, blk, b), reads=[kC_] + hreads,
                                  writes=[("ps", bC)])
                            P.add("pe", lambda e, blk=blkX, b=bX, mm=mm: mm(e, blk, b), reads=[kX_] + hreads,
                                  writes=[("ps", bX)])
                            P.add("pe", lambda e, blk=blkB, b=bB, mm=mm: mm(e, blk, b), reads=[kB_] + hreads,
                                  writes=[("ps", bB)])
                            ti = cnt["tmp"] % 2
                            cnt["tmp"] += 1
                            P.add("act", lambda e, b=bC, ti=ti: e.activation(
                                out=tmpA[ti][:], in_=ps[:, b, 0:TT], func=AF.Copy),
                                reads=[("ps", bC)], writes=[("tmpA", ti)])
                            t0 = tt * TT
                            P.add("dve", lambda e, b=bX, ti=ti, t0=t0: e.tensor_tensor(
                                out=zbuf[:, 2 + t0:2 + t0 + TT], in0=ps[:, b, 0:TT], in1=tmpA[ti][:],
                                op=ALU.mult),
                                reads=[("ps", bX), ("tmpA", ti)], writes=[("z", tt)])
                            zr = [("z", tt), "zpad"] + ([("z", tt - 1)] if tt > 0 else [])
                            c0 = wc0 + 0 * 8 + cc
                            c1 = wc0 + 1 * 8 + cc
                            c2 = wc0 + 2 * 8 + cc
                            P.add("act", lambda e, b=bC, t0=t0, c0=c0: e.activation(
                                out=ps[:, b, 0:TT], in_=zbuf[:, t0:t0 + TT], func=AF.Copy,
                                scale=smalls[:, c0:c0 + 1]),
                                reads=zr + ["smalls"], writes=[("ps", bC)])
                            P.add("dve", lambda e, b=bC, t0=t0, c1=c1: e.scalar_tensor_tensor(
                                out=ps[:, b, 0:TT], in0=zbuf[:, t0 + 1:t0 + 1 + TT], scalar=smalls[:, c1:c1 + 1],
                                in1=ps[:, b, 0:TT], op0=ALU.mult, op1=ALU.add),
                                reads=zr + ["smalls", ("ps", bC)], writes=[("ps", bC)])
                            P.add("dve", lambda e, b=bC, ti=ti, t0=t0, c2=c2: e.scalar_tensor_tensor(
                                out=tmpB[ti][:], in0=zbuf[:, t0 + 2:t0 + 2 + TT], scalar=smalls[:, c2:c2 + 1],
                                in1=ps[:, b, 0:TT], op0=ALU.mult, op1=ALU.add),
                                reads=zr + ["smalls", ("ps", bC)], writes=[("tmpB", ti)])
                            P.add("dve", lambda e, b=bB, ti=ti, cc=cc, tt=tt: e.tensor_tensor(
                                out=act[:, cc, tl(tt)], in0=ps[:, b, 0:TT], in1=tmpB[ti][:], op=ALU.mult),
                                reads=[("ps", bB), ("tmpB", ti)], writes=[("act", cc, tt)])
                add_block([(wv[:, :, D + cp * 256:D + (cp + 1) * 256], (KC, 256)),
                           (wv[:, :, 2 * D + cp * 256:2 * D + (cp + 1) * 256], (KC, 256)),
                           (wv[:, :, cp * 256:(cp + 1) * 256], (KC, 256))], compute)
            proj_residual(conv_w_out[j], KC, after)

        def sg_setup_early(j):
            def setup(views):
                P.add("sp", lambda e: e.dma_start(out=Rt[:], in_=sg_bs[j:j + 1, :].broadcast_to([128, D])),
                      writes=["Rt"], dma_key="rt")
                gi = state["stg"] % NSTG
                state["stg"] += 1
                sv = stgs[gi][:, 0:D].rearrange("p (g t) -> p g t", g=8)
                P.add("sp", lambda e: e.dma_start(out=stgs[gi][:, 0:D], in_=sg_wsT[j]),
                      writes=[("stg", gi)], dma_key="stg%d" % gi)
                P.add("pool", lambda e: e.affine_select(
                    out=sv, in_=sv, pattern=[[0, 8], [1, 128]], compare_op=ALU.is_ge, fill=0.0,
                    base=0, channel_multiplier=-1),
                    reads=[("stg", gi)], writes=[("stg", gi)])
                P.add("pool", lambda e: e.tensor_copy(out=wsT[:], in_=stgs[gi][:, 0:D]),
                      reads=[("stg", gi)], writes=["wsT"])
            add_block([], setup)

        def sg_mixer(j, after):
            wv = wview(sg_w_in[j])
            vg0 = C_VGAIN + j * 8
            vb0 = C_VBIAS + j * 8
            deferred = []

            def handover(views):
                P.add("pool", lambda e: e.memset(zbuf[:, 2 * D:2 * D + 2], 0.0), writes=SCR_KEYS)
                for half in range(2):
                    P.add("pe", lambda e, half=half: e.matmul(
                        ps[:, 4 + half, :], ones1[:], wsT[:, half * 512:(half + 1) * 512],
                        start=True, stop=True),
                        reads=["ones1", "wsT"], writes=[("ps", 4 + half)])
                for g in range(8):
                    b = 4 + g // 4
                    o = (g % 4) * 128
                    P.add("dve", lambda e, g=g, b=b, o=o: e.scalar_tensor_tensor(
                        out=Rt[:, g * 128:(g + 1) * 128], in0=ps[:, b, o:o + 128],
                        scalar=smalls[:, vb0 + g:vb0 + g + 1], in1=Rt[:, g * 128:(g + 1) * 128],
                        op0=ALU.mult, op1=ALU.add),
                        reads=[("ps", b), "smalls", "Rt"], writes=["Rt"])
            add_block([], handover)

            def vstage(views):
                vblk = {(0, 0): views[0], (0, 1): views[1], (1, 0): views[2], (1, 1): views[3]}

                def part1(tc):
                    pr = tc % 2
                    p3 = tc % 3
                    tt = (tc * 128) // TT
                    csl = slice(tc * 128, (tc + 1) * 128)
                    for cb in range(2):
                        b = 2 * p3 + cb

                        def mmv(e, cb=cb, b=b, csl=csl):
                            for kc in range(KC):
                                blk = vblk[(cb, kc // 4)][0]
                                i = e.matmul(ps[:, b, :], hn[:, kc, csl], blk[:, kc % 4, :],
                                             start=(kc == 0), stop=(kc == KC - 1))
                            return i
                        P.add("pe", mmv, reads=[vblk[(cb, 0)][1], vblk[(cb, 1)][1]] +
                              [("hn", kc, tt) for kc in range(KC)], writes=[("ps", b)])

                def part1b(tc):
                    pr = tc % 2
                    p3 = tc % 3
                    pv = ps[:, 2 * p3:2 * p3 + 2, :]
                    pkeys = [("ps", 2 * p3), ("ps", 2 * p3 + 1)]
                    for cb in range(2):
                        P.add("dve", lambda e, cb=cb, pr=pr, p3=p3: e.bn_stats(
                            out=stats[:, pr, cb, :], in_=ps[:, 2 * p3 + cb, :]),
                            reads=[pkeys[cb]], writes=[("stats", pr, cb)])
                    P.add("dve", lambda e, pr=pr: e.bn_aggr(
                        out=mv[:, pr, :], in_=stats[:, pr, :, :].rearrange("p a b -> p (a b)")),
                        reads=[("stats", pr, 0), ("stats", pr, 1)], writes=[("mv", pr)])
                    P.add("act", lambda e, pr=pr: e.activation(
                        out=lnr[:, pr, 0:1], in_=mv[:, pr, 1:2], func=AF.Ln, bias=epsr[:, 1:2]),
                        reads=[("mv", pr), "eps"], writes=[("lnr0", pr)])
                    P.add("act", lambda e, pr=pr: e.activation(
                        out=lnr[:, pr, 1:2], in_=lnr[:, pr, 0:1], func=AF.Exp, scale=-0.5),
                        reads=[("lnr0", pr)], writes=[("lnr1", pr)])

                def part1c(tc):
                    pr = tc % 2
                    p3 = tc % 3
                    pv = ps[:, 2 * p3:2 * p3 + 2, :]
                    pkeys = [("ps", 2 * p3), ("ps", 2 * p3 + 1)]
                    P.add("dve", lambda e, pr=pr: e.scalar_tensor_tensor(
                        out=lnr[:, pr, 2:3], in0=mv[:, pr, 0:1], scalar=-1.0, in1=lnr[:, pr, 1:2],
                        op0=ALU.mult, op1=ALU.mult),
                        reads=[("mv", pr), ("lnr1", pr)], writes=[("lnr2", pr)])
                    P.add("act", lambda e, pv=pv, pr=pr: e.activation(
                        out=vhat_b[pr].rearrange("p (a b) -> p a b", a=2), in_=pv, func=AF.Identity,
                        bias=lnr[:, pr, 2:3], scale=lnr[:, pr, 1:2]),
                        reads=pkeys + [("lnr1", pr), ("lnr2", pr)], writes=[("vhat", pr)])

                def part2(tc, mid=None):
                    pr = tc % 2
                    tt = (tc * 128) // TT
                    csl = slice(tc * 128, (tc + 1) * 128)
                    mb = 6

                    def mmg(e, pr=pr, mb=mb):
                        for g in range(8):
                            i = e.matmul(ps[:, mb + g // 4, (g % 4) * 128:(g % 4 + 1) * 128],
                                         vhat_b[pr][:, g * 128:(g + 1) * 128], wsT[:, g * 128:(g + 1) * 128],
                                         start=True, stop=True)
                        return i
                    P.add("pe", mmg, reads=[("vhat", pr), "wsT"], writes=[("ps", mb), ("ps", mb + 1)])
                    if mid is not None:
                        mid()
                    for g in range(8):
                        P.add("dve", lambda e, g=g, mb=mb, csl=csl: e.scalar_tensor_tensor(
                            out=act[:, g, csl], in0=ps[:, mb + g // 4, (g % 4) * 128:(g % 4 + 1) * 128],
                            scalar=smalls[:, vg0 + g:vg0 + g + 1], in1=Rt[:, g * 128:(g + 1) * 128],
                            op0=ALU.mult, op1=ALU.add),
                            reads=[("ps", mb + g // 4), "Rt", "smalls"], writes=[("act", g, tt)])

                for tc in range(NCH + 2):
                    if tc < NCH:
                        part1(tc)
                    if tc >= 2:
                        if tc >= NCH:
                            deferred.append(lambda c=tc - 2: part2(c))
                        else:
                            part2(tc - 2, mid=lambda tc=tc: part1b(tc))
                            part1c(tc)
                    elif tc < NCH:
                        part1b(tc)
                        part1c(tc)
            add_block([(wv[:, kh * 4:(kh + 1) * 4, D + cb * 512:D + (cb + 1) * 512], (4, 512))
                       for cb in range(2) for kh in range(2)], vstage)

            for cp in range(4):
                def compute(views, cp=cp):
                    blk, bkey = views[0]
                    for tt in range(NT):
                        for ci in range(2):
                            cc = cp * 2 + ci
                            if deferred and tt == NT - 1:
                                for f in deferred:
                                    f()
                                del deferred[:]
                            b = 2 * ((NCH - 3) % 3) + cnt["g"] % 2
                            cnt["g"] += 1

                            def mm(e, ci=ci, tt=tt, b=b):
                                for kc in range(KC):
                                    i = e.matmul(ps[:, b, 0:TT], blk[:, kc, ci * 128:(ci + 1) * 128],
                                                 hn[:, kc, tl(tt)], start=(kc == 0), stop=(kc == KC - 1))
                                return i
                            P.add("pe", mm, reads=[bkey] + [("hn", kc, tt) for kc in range(KC)],
                                  writes=[("ps", b)])
                            P.add("dve", lambda e, cc=cc, tt=tt, b=b: e.tensor_tensor(
                                out=act[:, cc, tl(tt)], in0=ps[:, b, 0:TT], in1=act[:, cc, tl(tt)],
                                op=ALU.mult),
                                reads=[("ps", b), ("act", cc, tt)], writes=[("act", cc, tt)])
                add_block([(wv[:, :, cp * 256:(cp + 1) * 256], (KC, 256))], compute)
            proj_residual(sg_w_out[j], KC, after)

        def ffn(L, after):
            wg = wview(ffn_w_gate[L])
            wu = wview(ffn_w_up[L])
            for (f0, f1) in ((0, 8), (8, 16), (16, 22)):
                fc = f0
                while fc < f1:
                    nb = min(2, f1 - fc)

                    def compute(views, fc=fc, nb=nb, f0=f0):
                        (blkG, kG), (blkU, kU) = views
                        for fi in range(nb):
                            for tt in range(NT):
                                g = cnt["g"] % 2
                                cnt["g"] += 1
                                bG, bU = g, 2 + g

                                def mm(e, blk, b, fi=fi, tt=tt):
                                    for kc in range(KC):
                                        i = e.matmul(ps[:, b, 0:TT], blk[:, kc, fi * 128:(fi + 1) * 128],
                                                     hn[:, kc, tl(tt)], start=(kc == 0), stop=(kc == KC - 1))
                                    return i
                                hreads = [("hn", kc, tt) for kc in range(KC)]
                                P.add("pe", lambda e, blk=blkG, b=bG, mm=mm: mm(e, blk, b),
                                      reads=[kG] + hreads, writes=[("ps", bG)])
                                P.add("pe", lambda e, blk=blkU, b=bU, mm=mm: mm(e, blk, b),
                                      reads=[kU] + hreads, writes=[("ps", bU)])
                                ti = cnt["tmp"] % 2
                                cnt["tmp"] += 1
                                P.add("act", lambda e, b=bG, ti=ti: e.activation(
                                    out=tmpA[ti][:], in_=ps[:, b, 0:TT], func=AF.Silu),
                                    reads=[("ps", bG)], writes=[("tmpA", ti)])
                                a = fc + fi - f0
                                P.add("dve", lambda e, b=bU, ti=ti, a=a, tt=tt: e.tensor_tensor(
                                    out=act[:, a, tl(tt)], in0=ps[:, b, 0:TT], in1=tmpA[ti][:], op=ALU.mult),
                                    reads=[("ps", bU), ("tmpA", ti)], writes=[("act", a, tt)])
                    add_block([(wg[:, :, fc * 128:(fc + nb) * 128], (KC, nb * 128)),
                               (wu[:, :, fc * 128:(fc + nb) * 128], (KC, nb * 128))], compute)
                    fc += nb
                if f1 == FC:
                    ple_p_load(L)
                proj_residual(ffn_w_down[L][f0 * 128:f1 * 128, :], f1 - f0, after if f1 == FC else None)

        def ple(L, after):
            pv = pT[L].rearrange("(kc p) t -> p kc t", p=128)
            wpv = ple_w_proj[L].rearrange("(kc p) n -> p kc n", p=128)
            wgv = wview(ple_w_gate[L])
            shared = {}

            def tile_compute(tt, pblk, kp):
                gate = shared["gate"]
                wpp, kpp = shared["wpp"]
                for dc in range(KC):
                    blk, bkey = gate[dc // 2]
                    di = dc % 2
                    g = cnt["g"] % 3
                    cnt["g"] += 1
                    bG, bP = g, 3 + g

                    def mm(e, blk=blk, di=di, b=bG):
                        for kc in range(KC):
                            i = e.matmul(ps[:, b, 0:TT], blk[:, kc, di * 128:(di + 1) * 128],
                                         hn[:, kc, tl(tt)], start=(kc == 0), stop=(kc == KC - 1))
                        return i
                    P.add("pe", mm, reads=[bkey] + [("hn", kc, tt) for kc in range(KC)], writes=[("ps", bG)])

                    def mmp(e, dc=dc, b=bP):
                        for k2 in range(2):
                            i = e.matmul(ps[:, b, 0:TT], wpp[:, k2, dc * 128:(dc + 1) * 128],
                                         pblk[:, k2, :], start=(k2 == 0), stop=(k2 == 1))
                        return i
                    P.add("pe", mmp, reads=[kpp, kp] + [("act", 6, tt), ("act", 7, tt)], writes=[("ps", bP)])
                    ti = cnt["tmp"] % 2
                    cnt["tmp"] += 1
                    P.add("act", lambda e, b=bG, ti=ti: e.activation(
                        out=tmpA[ti][:], in_=ps[:, b, 0:TT], func=AF.Sigmoid),
                        reads=[("ps", bG)], writes=[("tmpA", ti)])
                    P.add("dve", lambda e, b=bP, ti=ti: e.tensor_tensor(
                        out=ps[:, b, 0:TT], in0=ps[:, b, 0:TT], in1=tmpA[ti][:], op=ALU.mult),
                        reads=[("ps", bP), ("tmpA", ti)], writes=[("ps", bP)])
                    P.add("dve", lambda e, dc=dc, b=bP: e.tensor_tensor(
                        out=h[:, dc, tl(tt)], in0=ps[:, b, 0:TT], in1=h[:, dc, tl(tt)], op=ALU.add),
                        reads=[("ps", bP), ("h", dc, tt)], writes=[("h", dc, tt)])
                    interleave(after, tt, dc)

            def main(views):
                shared["gate"] = views[0:4]
                shared["wpp"] = views[4]
                tile_compute(0, act[:, 6:8, tl(0)], "p")
            add_block([(wgv[:, :, dp * 256:(dp + 1) * 256], (KC, 256)) for dp in range(4)] +
                      [(wpv, (2, D))], main, hold=NT - 1)
            for tt in range(1, NT):
                add_block([], lambda views, tt=tt: tile_compute(tt, act[:, 6:8, tl(tt)], "p"))

        def ple_p_load(L):
            pv = pT[L].rearrange("(kc p) t -> p kc t", p=128)

            def compute(views):
                for k2 in range(2):
                    load_to(pv[:, k2:k2 + 1, :].rearrange("p a (c t) -> p (a c) t", t=min(T, STG_EL)),
                            (T // min(T, STG_EL), min(T, STG_EL)), act[:, 6 + k2, :],
                            ["p"] + [("act", 6 + k2, tt) for tt in range(NT)])
            add_block([], compute)

        norm_block(Norm(C_MIX + layers[0] * 8))
        if layers[0] % 2 == 1:
            sg_setup_early(layers[0] // 2)
        for li, L in enumerate(layers):
            j = L // 2
            n_ffn = Norm(C_FFN + L * 8)
            n_ple = Norm(C_PLE + L * 8)
            if li + 1 < len(layers):
                n_next = Norm(C_MIX + layers[li + 1] * 8)
            elif final:
                n_next = Norm(C_FIN, to_out=True)
            else:
                n_next = None
            if L % 2 == 0:
                conv_mixer(j, n_ffn)
            else:
                sg_mixer(j, n_ffn)
            ffn(L, n_ple)
            if li + 1 < len(layers) and layers[li + 1] % 2 == 1:
                sg_setup_early(layers[li + 1] // 2)
            ple(L, n_next)
        if not final:
            def dump(views):
                for kc in range(KC):
                    P.add("sp", lambda e, kc=kc: e.dma_start(out=outT[kc * 128:(kc + 1) * 128, :], in_=h[:, kc, :]),
                          reads=[("h", kc, tt) for tt in range(NT)], writes=["out"], dma_key="outh%d" % kc)
            add_block([], dump)

        from collections import deque
        free = deque(range(NSLOT))
        release_at = {}
        nblk = len(blocks)
        flat = [(bi, li) for bi in range(nblk) for li in range(len(blocks[bi]["loads"]))]
        for b in blocks:
            b["views"] = [None] * len(b["loads"])
            b["slots"] = []
        nxt = 0

        def issue_one():
            nonlocal nxt
            bi, li = flat[nxt]
            b = blocks[bi]
            si = free.popleft()
            ap, shp = b["loads"][li]
            b["views"][li] = load_block(ap, shp, si)
            release_at.setdefault(bi + b["hold"], []).append(si)
            nxt += 1

        assert not blocks[0]["loads"]
        blocks[0]["compute"]([])
        for i in range(1, nblk):
            while nxt < len(flat) and flat[nxt][0] <= i:
                assert free, "not enough weight slots"
                issue_one()
            blocks[i]["compute"](blocks[i]["views"])
            for s in release_at.pop(i, []):
                free.append(s)
            while nxt < len(flat) and free:
                issue_one()

        P.add("sp", lambda e: e.nop(), reads=["out"] + [("out", kc, tt) for kc in range(KC) for tt in range(NT)],
              writes=["done"])

        with nc.Block() as block:
            P.finalize(block)
        P.close()
        PROG_STATS["ops"] = (len(P.ops), dict(P.eng_count), dict(P.sig_counts))
    return nc


def fm_cols(v):
    v = np.asarray(v, dtype=np.float32).reshape(-1, KC, 128)
    return np.ascontiguousarray(v.transpose(2, 0, 1).reshape(128, -1))


def make_in_maps(inp, T=SEQ, n_cores=N_CORES):
    f = lambda a: np.ascontiguousarray(np.asarray(a, dtype=np.float32))
    smalls = np.concatenate([
        fm_cols(inp["mix_norm"]), fm_cols(inp["ffn_norm"]), fm_cols(inp["ple_norm"]),
        fm_cols(inp["final_norm"]), fm_cols(np.asarray(inp["conv_w"]).reshape(-1, D)),
        fm_cols(inp["sg_v_bias"]), fm_cols(inp["sg_v_gain"])], axis=1)
    assert smalls.shape == (128, NS), smalls.shape
    shared = {
        "smalls": f(smalls),
        "conv_w_in": f(inp["conv_w_in"]), "conv_w_out": f(inp["conv_w_out"]),
        "sg_w_in": f(inp["sg_w_in"]), "sg_w_out": f(inp["sg_w_out"]),
        "sg_bs": f(np.asarray(inp["sg_b_spatial"]).reshape(2, D)),
        "sg_wsT": f(np.asarray(inp["sg_w_spatial"]).transpose(0, 3, 1, 2).reshape(2, 128, D)),
        "ffn_w_gate": f(inp["ffn_w_gate"]), "ffn_w_up": f(inp["ffn_w_up"]),
        "ffn_w_down": f(inp["ffn_w_down"]),
        "ple_w_gate": f(inp["ple_w_gate"]), "ple_w_proj": f(inp["ple_w_proj"]),
    }
    x = np.asarray(inp["x"], dtype=np.float32)
    p = np.asarray(inp["p"], dtype=np.float32)
    maps = []
    for c in range(n_cores):
        m = dict(shared)
        m["xT"] = np.ascontiguousarray(x[c].T)
        m["pT"] = np.ascontiguousarray(p[:, c].transpose(0, 2, 1))
        maps.append(m)
    return maps


_NC_CACHE = {}


def kernel(**inputs):
    x = np.asarray(inputs["x"])
    B, T, _ = x.shape
    assert B == N_CORES and T == SEQ
    if "nc" not in _NC_CACHE:
        _NC_CACHE["nc"] = build_program(T)
    nc = _NC_CACHE["nc"]
    in_maps = make_in_maps(inputs, T, N_CORES)
    res = run_bass_kernel_spmd(nc, in_maps, core_ids=list(range(N_CORES)))
    out = np.stack([np.asarray(r["outT"]).T for r in res.results], axis=0)
    return np.ascontiguousarray(out.astype(np.float32))
```

```python
from contextlib import ExitStack

import numpy as np
import concourse.bass as bass
import concourse.mybir as mybir
from concourse.bass_utils import run_bass_kernel_spmd

F32 = mybir.dt.float32
BF16 = mybir.dt.bfloat16
ALU = mybir.AluOpType
AF = mybir.ActivationFunctionType

D = 1024
KC = 8
DFF = 2816
FC = 22
PLE = 256
DEPTH = 4
RMS_EPS = 1e-6
LN_EPS = 1e-5
N_CORES = 8
SEQ = 2048

C_MIX = 0
C_FFN = 32
C_PLE = 64
C_FIN = 96
C_CONVW = 104
C_VBIAS = 152
C_VGAIN = 168
NS = 184

ENGS = ("pe", "act", "dve", "pool", "sp")
PROG_STATS = {}


class Op:
    __slots__ = ("eng", "emit", "deps", "signal", "seq", "dma_sem", "dma_val", "idx", "eidx")

    def __init__(self, eng, emit):
        self.eng = eng
        self.emit = emit
        self.deps = []
        self.signal = False
        self.seq = 0
        self.dma_sem = None
        self.dma_val = 0
        self.idx = 0
        self.eidx = 0


class Prog:
    NEAR = 10 ** 9

    def __init__(self, nc):
        self.nc = nc
        self.ops = []
        self.last_w = {}
        self.readers = {}
        self.eng_count = {e: 0 for e in ENGS}
        self.dma_sems = {}
        self._sem_ctx = []

    def _new_sem(self, name):
        cm = self.nc.semaphore(name)
        s = cm.__enter__()
        self._sem_ctx.append(cm)
        return s

    def add(self, eng, emit, reads=(), writes=(), dma_key=None):
        op = Op(eng, emit)
        op.idx = len(self.ops)
        op.eidx = self.eng_count[eng]
        self.eng_count[eng] += 1
        deps = {}

        def dep(o):
            if o is None or o is op:
                return
            deps[o.idx] = o

        for r in reads:
            dep(self.last_w.get(r))
        for w in writes:
            dep(self.last_w.get(w))
            for rd in self.readers.get(w, ()):
                dep(rd)
        for w in writes:
            self.last_w[w] = op
            self.readers[w] = []
        for r in reads:
            if r in writes:
                continue
            lst = self.readers.setdefault(r, [])
            if dma_key is None:
                lst[:] = [o for o in lst if not (o.eng == eng and o.dma_sem is None)]
            lst.append(op)
        if dma_key is not None:
            ent = self.dma_sems.get(dma_key)
            if ent is None:
                ent = [self._new_sem("dq_%s" % dma_key), 0]
                self.dma_sems[dma_key] = ent
            ent[1] += 16
            op.dma_sem = ent[0]
            op.dma_val = ent[1]
        op.deps = list(deps.values())
        self.ops.append(op)
        return op

    def _skip_same_engine(self, op, d):
        if d.eng != op.eng or d.dma_sem is not None:
            return False
        if d.eng == "pe":
            return True
        return op.eidx - d.eidx > self.NEAR

    def finalize(self, block):
        for op in self.ops:
            for d in op.deps:
                if d.dma_sem is None and not self._skip_same_engine(op, d):
                    d.signal = True
        esem = {e: self._new_sem("es_" + e) for e in ENGS}
        cnt = {e: 0 for e in ENGS}
        for op in self.ops:
            if op.dma_sem is None and op.signal:
                cnt[op.eng] += 1
                op.seq = cnt[op.eng]
        self.sig_counts = dict(cnt)
        per_eng = {e: [o for o in self.ops if o.eng == e] for e in ENGS}

        def run(engname, eh):
            known = {}
            for op in per_eng[engname]:
                need = {}
                for d in op.deps:
                    if d.dma_sem is not None:
                        key = ("d", id(d.dma_sem))
                        sem, val = d.dma_sem, d.dma_val
                    else:
                        if self._skip_same_engine(op, d):
                            continue
                        key = ("e", d.eng)
                        sem, val = esem[d.eng], d.seq
                    if known.get(key, 0) >= val:
                        continue
                    if key not in need or need[key][1] < val:
                        need[key] = (sem, val)
                for key, (sem, val) in need.items():
                    eh.wait_ge(sem, val)
                    known[key] = val
                inst = op.emit(eh)
                if op.dma_sem is not None:
                    inst.then_inc(op.dma_sem, 16)
                elif op.signal:
                    inst.then_inc(esem[engname], 1)

        @block.tensor
        def _(e):
            run("pe", e)

        @block.scalar
        def _(e):
            run("act", e)

        @block.vector
        def _(e):
            run("dve", e)

        @block.gpsimd
        def _(e):
            run("pool", e)

        @block.sync
        def _(e):
            run("sp", e)

    def close(self):
        for cm in reversed(self._sem_ctx):
            cm.__exit__(None, None, None)
        self._sem_ctx = []


def build_program(T=SEQ, layers=(0, 1, 2, 3), final=True):
    TT = min(512, T)
    NT = T // TT
    NCH = T // 128
    nc = bass.Bass("TRN2", target_bir_lowering=False)

    def din(name, shape):
        return nc.dram_tensor(name, list(shape), F32, kind="ExternalInput").ap()

    xT = din("xT", [D, T])
    pT = din("pT", [DEPTH, PLE, T])
    smalls_d = din("smalls", [128, NS])
    conv_w_in = din("conv_w_in", [2, D, 3 * D])
    conv_w_out = din("conv_w_out", [2, D, D])
    sg_w_in = din("sg_w_in", [2, D, 2 * D])
    sg_w_out = din("sg_w_out", [2, D, D])
    sg_bs = din("sg_bs", [2, D])
    sg_wsT = din("sg_wsT", [2, 128, D])
    ffn_w_gate = din("ffn_w_gate", [DEPTH, D, DFF])
    ffn_w_up = din("ffn_w_up", [DEPTH, D, DFF])
    ffn_w_down = din("ffn_w_down", [DEPTH, DFF, D])
    ple_w_gate = din("ple_w_gate", [DEPTH, D, D])
    ple_w_proj = din("ple_w_proj", [DEPTH, PLE, D])
    outT = nc.dram_tensor("outT", [D, T], F32, kind="ExternalOutput").ap()

    NSLOT = 10
    NSTG = 3
    SLOT_EL = 2048
    STG_EL = 1024

    with ExitStack() as es:
        def sb(name, shape, dt):
            return es.enter_context(nc.sbuf_tensor(name, list(shape), dt))

        h = sb("h", [128, KC, T], F32)
        hn = sb("hn", [128, KC, T], BF16)
        act = sb("act", [128, KC, T], BF16)
        slots = [sb("slot%d" % i, [128, SLOT_EL], BF16) for i in range(NSLOT)]
        stgs = [sb("stg%d" % i, [128, STG_EL], F32) for i in range(NSTG)]
        smalls = sb("smalls_sb", [128, NS], F32)
        onesD = sb("onesD", [128, 128], BF16)
        ones1 = sb("ones1", [128, 128], BF16)
        rstd_t = [sb("rstd%d" % i, [128, TT], F32) for i in range(1)]
        NSQ = 8
        tmpA = [sb("tmpA%d" % i, [128, TT], F32) for i in range(2)]
        tmpB = [sb("tmpB%d" % i, [128, TT], F32) for i in range(2)]
        zbuf = sb("zbuf", [128, 2 + 2048], F32)
        vhat_all = zbuf[:, 0:D].bitcast(BF16)
        vhat_b = [vhat_all[:, 0:D], vhat_all[:, D:2 * D]]
        sq_all = zbuf[:, 0:2048].bitcast(BF16)
        sq_t = [sq_all[:, i * 512:i * 512 + TT] for i in range(NSQ)]
        SCR_KEYS = (["zpad", ("vhat", 0), ("vhat", 1)] + [("z", tt) for tt in range(NT)] +
                    [("sq", i) for i in range(NSQ)])
        ACT_KEYS = [("act", j, tt) for j in range(KC) for tt in range(NT)]
        Rt = sb("Rt", [128, D], F32)
        wsT = sb("wsT", [128, D], BF16)
        stats = sb("stats", [128, 2, 2, 6], F32)
        mv = sb("mv", [128, 2, 2], F32)
        lnr = sb("lnr", [128, 2, 4], F32)
        epsr = sb("epsr", [128, 2], F32)
        ps = es.enter_context(nc.psum_tensor("ps", [128, 8, 512], F32))

        P = Prog(nc)
        blocks = []

        def add_block(loads, compute, hold=0):
            blocks.append({"loads": loads, "compute": compute, "hold": hold})

        def tl(tt):
            return slice(tt * TT, (tt + 1) * TT)

        P.add("pool", lambda e: e.memset(onesD[:], 1.0 / D), writes=["onesD"])
        P.add("pool", lambda e: e.memset(ones1[:], 1.0), writes=["ones1"])
        P.add("pool", lambda e: e.memset(epsr[:, 0:1], RMS_EPS), writes=["eps"])
        P.add("pool", lambda e: e.memset(epsr[:, 1:2], LN_EPS), writes=["eps"])
        P.add("sp", lambda e: e.dma_start(out=smalls[:], in_=smalls_d), writes=["smalls"], dma_key="sm")
        xv = xT.rearrange("(kc p) t -> p kc t", p=128)

        def load_x(tt):
            P.add("sp", lambda e, tt=tt: e.dma_start(out=h[:, :, tl(tt)], in_=xv[:, :, tl(tt)]),
                  writes=[("h", kc, tt) for kc in range(KC)], dma_key="x%d" % tt)
        load_x(0)

        state = {"slot": 0, "stg": 0, "cast": 0}
        cast_engs = ("pool",)

        def load_to(src_ap, shape, dst2d, wkeys):
            a, b = shape
            assert b <= STG_EL
            rows = max(1, STG_EL // b)
            a0 = 0
            while a0 < a:
                a1 = min(a, a0 + rows)
                m = (a1 - a0) * b
                gi = state["stg"] % NSTG
                state["stg"] += 1
                stg_v = stgs[gi][:, 0:m].rearrange("p (a b) -> p a b", a=a1 - a0)
                P.add("sp", lambda e, stg_v=stg_v, a0=a0, a1=a1: e.dma_start(out=stg_v, in_=src_ap[:, a0:a1, :]),
                      writes=[("stg", gi)], dma_key="stg%d" % gi)
                dst = dst2d[:, a0 * b:a0 * b + m]
                srcv = stgs[gi][:, 0:m]
                P.add("pool", lambda e, dst=dst, srcv=srcv: e.tensor_copy(out=dst, in_=srcv),
                      reads=[("stg", gi)], writes=wkeys)
                a0 = a1

        def load_block(src_ap, shape, si):
            a, b = shape
            n = a * b
            assert n <= SLOT_EL and b <= STG_EL
            rows = max(1, STG_EL // b)
            a0 = 0
            while a0 < a:
                a1 = min(a, a0 + rows)
                m = (a1 - a0) * b
                gi = state["stg"] % NSTG
                state["stg"] += 1
                ce = cast_engs[state["cast"] % len(cast_engs)]
                state["cast"] += 1
                stg_v = stgs[gi][:, 0:m].rearrange("p (a b) -> p a b", a=a1 - a0)
                P.add("sp", lambda e, stg_v=stg_v, a0=a0, a1=a1: e.dma_start(out=stg_v, in_=src_ap[:, a0:a1, :]),
                      writes=[("stg", gi)], dma_key="stg%d" % gi)
                dst = slots[si][:, a0 * b:a0 * b + m]
                srcv = stgs[gi][:, 0:m]
                if ce == "act":
                    P.add("act", lambda e, dst=dst, srcv=srcv: e.activation(out=dst, in_=srcv, func=AF.Copy),
                          reads=[("stg", gi)], writes=[("slot", si)])
                else:
                    P.add(ce, lambda e, dst=dst, srcv=srcv: e.tensor_copy(out=dst, in_=srcv),
                          reads=[("stg", gi)], writes=[("slot", si)])
                a0 = a1
            slot_v = slots[si][:, 0:n].rearrange("p (a b) -> p a b", a=a)
            return slot_v, ("slot", si)

        def wview(w2d):
            return w2d.rearrange("(kc p) n -> p kc n", p=128)

        cnt = {"sq": 0, "rstd": 0, "hn": 0, "tmp": 0, "g": 0, "out": 0}
        otiles = []
        per = (T * 2) // (4 * TT)
        for jj in range(6):
            a32 = act[:, jj, :].bitcast(F32)
            for hh in range(per):
                tts = sorted(set(((hh * TT * 2 + o) // TT) for o in range(0, 2 * TT, TT)) & set(range(NT)))
                otiles.append((a32[:, hh * TT:(hh + 1) * TT], [("act", jj, t_) for t_ in tts]))

        class Norm:
            def __init__(self, gcol0, to_out=False):
                self.gcol0 = gcol0
                self.to_out = to_out
                self.ri = {}

            def square_one(self, tt, kc):
                wr = [("sq", kc)] + (SCR_KEYS if kc == 0 else [])
                P.add("act", lambda e: e.activation(out=sq_t[kc], in_=h[:, kc, tl(tt)], func=AF.Square),
                      reads=[("h", kc, tt)], writes=wr)

            def squares(self, tt):
                for kc in range(KC):
                    self.square_one(tt, kc)

            def stats(self, tt):
                for kc in range(KC):
                    P.add("pe", lambda e, kc=kc: e.matmul(
                        ps[:, 7, 0:TT], onesD[:], sq_t[kc], start=(kc == 0), stop=(kc == KC - 1)),
                        reads=[("sq", kc), "onesD"], writes=[("ps", 7)])
                ri = 0
                cnt["rstd"] += 1
                self.ri[tt] = ri
                P.add("act", lambda e, ri=ri: e.activation(
                    out=rstd_t[ri][:], in_=ps[:, 7, 0:TT], func=AF.Ln, bias=epsr[:, 0:1]),
                    reads=[("ps", 7), "eps"], writes=[("rstd", ri)])
                P.add("act", lambda e, ri=ri: e.activation(
                    out=ps[:, 6, 0:TT], in_=rstd_t[ri][:], func=AF.Exp, scale=-0.5),
                    reads=[("rstd", ri)], writes=[("ps", 6)])

            def apply(self, tt, kcs):
                ri = self.ri[tt]
                gcol0 = self.gcol0
                for kc in kcs:
                    if not self.to_out:
                        P.add("dve", lambda e, kc=kc, tt=tt, ri=ri: e.scalar_tensor_tensor(
                            out=hn[:, kc, tl(tt)], in0=h[:, kc, tl(tt)],
                            scalar=smalls[:, gcol0 + kc:gcol0 + kc + 1], in1=ps[:, 6, 0:TT],
                            op0=ALU.mult, op1=ALU.mult),
                            reads=[("h", kc, tt), ("ps", 6), "smalls"], writes=[("hn", kc, tt)])
                    else:
                        oi = cnt["out"] % len(otiles)
                        cnt["out"] += 1
                        oap, okeys = otiles[oi]
                        P.add("dve", lambda e, kc=kc, tt=tt, oap=oap: e.scalar_tensor_tensor(
                            out=oap, in0=h[:, kc, tl(tt)],
                            scalar=smalls[:, gcol0 + kc:gcol0 + kc + 1], in1=ps[:, 6, 0:TT],
                            op0=ALU.mult, op1=ALU.mult),
                            reads=[("h", kc, tt), ("ps", 6), "smalls"], writes=[("otile", oi)] + okeys)
                        P.add("sp", lambda e, kc=kc, tt=tt, oap=oap: e.dma_start(
                            out=outT[kc * 128:(kc + 1) * 128, tl(tt)], in_=oap),
                            reads=[("otile", oi)], writes=[("out", kc, tt)], dma_key="out%d" % oi)

            def whole_tile(self, tt):
                self.squares(tt)
                self.stats(tt)
                self.apply(tt, range(KC))

        def norm_block(norm):
            def compute(views):
                for tt in range(NT):
                    norm.whole_tile(tt)
            add_block([], compute)

        APPLY_AT = {3: (0,), 4: (1, 2), 5: (3, 4), 6: (5, 6), 7: (7,)}
        SQ_LAG = 2

        def interleave(after, tt, dc):
            if after is None:
                return
            n = tt * KC + dc
            m = n - SQ_LAG
            if m >= 0:
                after.square_one(m // KC, m % KC)
            if tt >= 1:
                if dc == SQ_LAG - 1:
                    after.stats(tt - 1)
                elif dc in APPLY_AT:
                    after.apply(tt - 1, APPLY_AT[dc])
            if dc == KC - 1 and tt == NT - 1:
                for m in range(n - SQ_LAG + 1, n + 1):
                    after.square_one(m // KC, m % KC)
                after.stats(tt)
                after.apply(tt, range(KC))

        def proj_residual(w2d, nk, after=None):
            wv = wview(w2d)

            def compute(views):
                for tt in range(NT):
                    for dc in range(KC):
                        blk, bkey = views[dc // 2]
                        di = dc % 2
                        b = 4 + (cnt["g"] % 2)
                        cnt["g"] += 1

                        def mm(e, blk=blk, di=di, tt=tt, b=b):
                            for j in range(nk):
                                i = e.matmul(ps[:, b, 0:TT], blk[:, j, di * 128:(di + 1) * 128],
                                             act[:, j, tl(tt)], start=(j == 0), stop=(j == nk - 1))
                            return i
                        P.add("pe", mm, reads=[bkey] + [("act", j, tt) for j in range(nk)],
                              writes=[("ps", b)])
                        P.add("dve", lambda e, dc=dc, tt=tt, b=b: e.tensor_tensor(
                            out=h[:, dc, tl(tt)], in0=ps[:, b, 0:TT], in1=h[:, dc, tl(tt)], op=ALU.add),
                            reads=[("ps", b), ("h", dc, tt)], writes=[("h", dc, tt)])
                        interleave(after, tt, dc)
            add_block([(wv[:, :, dp * 256:(dp + 1) * 256], (nk, 256)) for dp in range(4)], compute)

        def conv_mixer(j, after):
            wv = wview(conv_w_in[j])
            wc0 = C_CONVW + j * 24

            def setup(views):
                P.add("pool", lambda e: e.memset(zbuf[:, 0:2], 0.0), writes=SCR_KEYS)
            add_block([], setup)
            for cp in range(4):
                def compute(views, cp=cp):
                    (blkC, kC_), (blkX, kX_), (blkB, kB_) = views
                    for ci in range(2):
                        cc = cp * 2 + ci
                        for tt in range(NT):
                            g = cnt["g"] % 2
                            cnt["g"] += 1
                            bC, bX, bB = 3 * g, 3 * g + 1, 3 * g + 2

                            def mm(e, blk, b, ci=ci, tt=tt):
                                for kc in range(KC):
                                    i = e.matmul(ps[:, b, 0:TT], blk[:, kc, ci * 128:(ci + 1) * 128],
                                                 hn[:, kc, tl(tt)], start=(kc == 0), stop=(kc == KC - 1))
                                return i
                            hreads = [("hn", kc, tt) for kc in range(KC)]
                            P.add("pe", lambda e, blk=blkC, b=bC, mm=mm: mm(e, blk, b), reads=[kC_] + hreads,
                                  writes=[("ps", bC)])
                            P.add("pe", lambda e, blk=blkX, b=bX, mm=mm: mm(e, blk, b), reads=[kX_] + hreads,
                                  writes=[("ps", bX)])
                            P.add("pe", lambda e, blk=blkB, b=bB, mm=mm: mm(e, blk, b), reads=[kB_] + hreads,
                                  writes=[("ps", bB)])
                            ti = cnt["tmp"] % 2
                            cnt["tmp"] += 1
                            P.add("act", lambda e, b=bC, ti=ti: e.activation(
                                out=tmpA[ti][:], in_=ps[:, b, 0:TT], func=AF.Copy),
                                reads=[("ps", bC)], writes=[("tmpA", ti)])
                            t0 = tt * TT
                            P.add("dve", lambda e, b=bX, ti=ti, t0=t0: e.tensor_tensor(
                                out=zbuf[:, 2 + t0:2 + t0 + TT], in0=ps[:, b, 0:TT], in1=tmpA[ti][:],
                                op=ALU.mult),
                                reads=[("ps", bX), ("tmpA", ti)], writes=[("z", tt)])
                            zr = [("z", tt), "zpad"] + ([("z", tt - 1)] if tt > 0 else [])
                            c0 = wc0 + 0 * 8 + cc
                            c1 = wc0 + 1 * 8 + cc
                            c2 = wc0 + 2 * 8 + cc
                            P.add("act", lambda e, b=bC, t0=t0, c0=c0: e.activation(
                                out=ps[:, b, 0:TT], in_=zbuf[:, t0:t0 + TT], func=AF.Copy,
                                scale=smalls[:, c0:c0 + 1]),
                                reads=zr + ["smalls"], writes=[("ps", bC)])
                            P.add("dve", lambda e, b=bC, t0=t0, c1=c1: e.scalar_tensor_tensor(
                                out=ps[:, b, 0:TT], in0=zbuf[:, t0 + 1:t0 + 1 + TT], scalar=smalls[:, c1:c1 + 1],
                                in1=ps[:, b, 0:TT], op0=ALU.mult, op1=ALU.add),
                                reads=zr + ["smalls", ("ps", bC)], writes=[("ps", bC)])
                            P.add("dve", lambda e, b=bC, ti=ti, t0=t0, c2=c2: e.scalar_tensor_tensor(
                                out=tmpB[ti][:], in0=zbuf[:, t0 + 2:t0 + 2 + TT], scalar=smalls[:, c2:c2 + 1],
                                in1=ps[:, b, 0:TT], op0=ALU.mult, op1=ALU.add),
                                reads=zr + ["smalls", ("ps", bC)], writes=[("tmpB", ti)])
                            P.add("dve", lambda e, b=bB, ti=ti, cc=cc, tt=tt: e.tensor_tensor(
                                out=act[:, cc, tl(tt)], in0=ps[:, b, 0:TT], in1=tmpB[ti][:], op=ALU.mult),
                                reads=[("ps", bB), ("tmpB", ti)], writes=[("act", cc, tt)])
                add_block([(wv[:, :, D + cp * 256:D + (cp + 1) * 256], (KC, 256)),
                           (wv[:, :, 2 * D + cp * 256:2 * D + (cp + 1) * 256], (KC, 256)),
                           (wv[:, :, cp * 256:(cp + 1) * 256], (KC, 256))], compute)
            proj_residual(conv_w_out[j], KC, after)

        def sg_setup_early(j):
            def setup(views):
                P.add("sp", lambda e: e.dma_start(out=Rt[:], in_=sg_bs[j:j + 1, :].broadcast_to([128, D])),
                      writes=["Rt"], dma_key="rt")
                gi = state["stg"] % NSTG
                state["stg"] += 1
                sv = stgs[gi][:, 0:D].rearrange("p (g t) -> p g t", g=8)
                P.add("sp", lambda e: e.dma_start(out=stgs[gi][:, 0:D], in_=sg_wsT[j]),
                      writes=[("stg", gi)], dma_key="stg%d" % gi)
                P.add("pool", lambda e: e.affine_select(
                    out=sv, in_=sv, pattern=[[0, 8], [1, 128]], compare_op=ALU.is_ge, fill=0.0,
                    base=0, channel_multiplier=-1),
                    reads=[("stg", gi)], writes=[("stg", gi)])
                P.add("pool", lambda e: e.tensor_copy(out=wsT[:], in_=stgs[gi][:, 0:D]),
                      reads=[("stg", gi)], writes=["wsT"])
            add_block([], setup)

        def sg_mixer(j, after):
            wv = wview(sg_w_in[j])
            vg0 = C_VGAIN + j * 8
            vb0 = C_VBIAS + j * 8
            deferred = []

            def handover(views):
                P.add("pool", lambda e: e.memset(zbuf[:, 2 * D:2 * D + 2], 0.0), writes=SCR_KEYS)
                for half in range(2):
                    P.add("pe", lambda e, half=half: e.matmul(
                        ps[:, 4 + half, :], ones1[:], wsT[:, half * 512:(half + 1) * 512],
                        start=True, stop=True),
                        reads=["ones1", "wsT"], writes=[("ps", 4 + half)])
                for g in range(8):
                    b = 4 + g // 4
                    o = (g % 4) * 128
                    P.add("dve", lambda e, g=g, b=b, o=o: e.scalar_tensor_tensor(
                        out=Rt[:, g * 128:(g + 1) * 128], in0=ps[:, b, o:o + 128],
                        scalar=smalls[:, vb0 + g:vb0 + g + 1], in1=Rt[:, g * 128:(g + 1) * 128],
                        op0=ALU.mult, op1=ALU.add),
                        reads=[("ps", b), "smalls", "Rt"], writes=["Rt"])
            add_block([], handover)

            def vstage(views):
                vblk = {(0, 0): views[0], (0, 1): views[1], (1, 0): views[2], (1, 1): views[3]}

                def part1(tc):
                    pr = tc % 2
                    p3 = tc % 3
                    tt = (tc * 128) // TT
                    csl = slice(tc * 128, (tc + 1) * 128)
                    for cb in range(2):
                        b = 2 * p3 + cb

                        def mmv(e, cb=cb, b=b, csl=csl):
                            for kc in range(KC):
                                blk = vblk[(cb, kc // 4)][0]
                                i = e.matmul(ps[:, b, :], hn[:, kc, csl], blk[:, kc % 4, :],
                                             start=(kc == 0), stop=(kc == KC - 1))
                            return i
                        P.add("pe", mmv, reads=[vblk[(cb, 0)][1], vblk[(cb, 1)][1]] +
                              [("hn", kc, tt) for kc in range(KC)], writes=[("ps", b)])

                def part1b(tc):
                    pr = tc % 2
                    p3 = tc % 3
                    pv = ps[:, 2 * p3:2 * p3 + 2, :]
                    pkeys = [("ps", 2 * p3), ("ps", 2 * p3 + 1)]
                    for cb in range(2):
                        P.add("dve", lambda e, cb=cb, pr=pr, p3=p3: e.bn_stats(
                            out=stats[:, pr, cb, :], in_=ps[:, 2 * p3 + cb, :]),
                            reads=[pkeys[cb]], writes=[("stats", pr, cb)])
                    P.add("dve", lambda e, pr=pr: e.bn_aggr(
                        out=mv[:, pr, :], in_=stats[:, pr, :, :].rearrange("p a b -> p (a b)")),
                        reads=[("stats", pr, 0), ("stats", pr, 1)], writes=[("mv", pr)])
                    P.add("act", lambda e, pr=pr: e.activation(
                        out=lnr[:, pr, 0:1], in_=mv[:, pr, 1:2], func=AF.Ln, bias=epsr[:, 1:2]),
                        reads=[("mv", pr), "eps"], writes=[("lnr0", pr)])
                    P.add("act", lambda e, pr=pr: e.activation(
                        out=lnr[:, pr, 1:2], in_=lnr[:, pr, 0:1], func=AF.Exp, scale=-0.5),
                        reads=[("lnr0", pr)], writes=[("lnr1", pr)])

                def part1c(tc):
                    pr = tc % 2
                    p3 = tc % 3
                    pv = ps[:, 2 * p3:2 * p3 + 2, :]
                    pkeys = [("ps", 2 * p3), ("ps", 2 * p3 + 1)]
                    P.add("dve", lambda e, pr=pr: e.scalar_tensor_tensor(
                        out=lnr[:, pr, 2:3], in0=mv[:, pr, 0:1], scalar=-1.0, in1=lnr[:, pr, 1:2],
                        op0=ALU.mult, op1=ALU.mult),
                        reads=[("mv", pr), ("lnr1", pr)], writes=[("lnr2", pr)])
                    P.add("act", lambda e, pv=pv, pr=pr: e.activation(
                        out=vhat_b[pr].rearrange("p (a b) -> p a b", a=2), in_=pv, func=AF.Identity,
                        bias=lnr[:, pr, 2:3], scale=lnr[:, pr, 1:2]),
                        reads=pkeys + [("lnr1", pr), ("lnr2", pr)], writes=[("vhat", pr)])

                def part2(tc, mid=None):
                    pr = tc % 2
                    tt = (tc * 128) // TT
                    csl = slice(tc * 128, (tc + 1) * 128)
                    mb = 6

                    def mmg(e, pr=pr, mb=mb):
                        for g in range(8):
                            i = e.matmul(ps[:, mb + g // 4, (g % 4) * 128:(g % 4 + 1) * 128],
                                         vhat_b[pr][:, g * 128:(g + 1) * 128], wsT[:, g * 128:(g + 1) * 128],
                                         start=True, stop=True)
                        return i
                    P.add("pe", mmg, reads=[("vhat", pr), "wsT"], writes=[("ps", mb), ("ps", mb + 1)])
                    if mid is not None:
                        mid()
                    for g in range(8):
                        P.add("dve", lambda e, g=g, mb=mb, csl=csl: e.scalar_tensor_tensor(
                            out=act[:, g, csl], in0=ps[:, mb + g // 4, (g % 4) * 128:(g % 4 + 1) * 128],
                            scalar=smalls[:, vg0 + g:vg0 + g + 1], in1=Rt[:, g * 128:(g + 1) * 128],
                            op0=ALU.mult, op1=ALU.add),
                            reads=[("ps", mb + g // 4), "Rt", "smalls"], writes=[("act", g, tt)])

                for tc in range(NCH + 2):
                    if tc < NCH:
                        part1(tc)
                    if tc >= 2:
                        if tc >= NCH:
                            deferred.append(lambda c=tc - 2: part2(c))
                        else:
                            part2(tc - 2, mid=lambda tc=tc: part1b(tc))
                            part1c(tc)
                    elif tc < NCH:
                        part1b(tc)
                        part1c(tc)
            add_block([(wv[:, kh * 4:(kh + 1) * 4, D + cb * 512:D + (cb + 1) * 512], (4, 512))
                       for cb in range(2) for kh in range(2)], vstage)

            for cp in range(4):
                def compute(views, cp=cp):
                    blk, bkey = views[0]
                    for tt in range(NT):
                        for ci in range(2):
                            cc = cp * 2 + ci
                            if deferred and tt == NT - 1:
                                for f in deferred:
                                    f()
                                del deferred[:]
                            b = 2 * ((NCH - 3) % 3) + cnt["g"] % 2
                            cnt["g"] += 1

                            def mm(e, ci=ci, tt=tt, b=b):
                                for kc in range(KC):
                                    i = e.matmul(ps[:, b, 0:TT], blk[:, kc, ci * 128:(ci + 1) * 128],
                                                 hn[:, kc, tl(tt)], start=(kc == 0), stop=(kc == KC - 1))
                                return i
                            P.add("pe", mm, reads=[bkey] + [("hn", kc, tt) for kc in range(KC)],
                                  writes=[("ps", b)])
                            P.add("dve", lambda e, cc=cc, tt=tt, b=b: e.tensor_tensor(
                                out=act[:, cc, tl(tt)], in0=ps[:, b, 0:TT], in1=act[:, cc, tl(tt)],
                                op=ALU.mult),
                                reads=[("ps", b), ("act", cc, tt)], writes=[("act", cc, tt)])
                add_block([(wv[:, :, cp * 256:(cp + 1) * 256], (KC, 256))], compute)
            proj_residual(sg_w_out[j], KC, after)

        def ffn(L, after):
            wg = wview(ffn_w_gate[L])
            wu = wview(ffn_w_up[L])
            for (f0, f1) in ((0, 8), (8, 16), (16, 22)):
                fc = f0
                while fc < f1:
                    nb = min(2, f1 - fc)

                    def compute(views, fc=fc, nb=nb, f0=f0):
                        (blkG, kG), (blkU, kU) = views
                        for fi in range(nb):
                            for tt in range(NT):
                                g = cnt["g"] % 2
                                cnt["g"] += 1
                                bG, bU = g, 2 + g

                                def mm(e, blk, b, fi=fi, tt=tt):
                                    for kc in range(KC):
                                        i = e.matmul(ps[:, b, 0:TT], blk[:, kc, fi * 128:(fi + 1) * 128],
                                                     hn[:, kc, tl(tt)], start=(kc == 0), stop=(kc == KC - 1))
                                    return i
                                hreads = [("hn", kc, tt) for kc in range(KC)]
                                P.add("pe", lambda e, blk=blkG, b=bG, mm=mm: mm(e, blk, b),
                                      reads=[kG] + hreads, writes=[("ps", bG)])
                                P.add("pe", lambda e, blk=blkU, b=bU, mm=mm: mm(e, blk, b),
                                      reads=[kU] + hreads, writes=[("ps", bU)])
                                ti = cnt["tmp"] % 2
                                cnt["tmp"] += 1
                                P.add("act", lambda e, b=bG, ti=ti: e.activation(
                                    out=tmpA[ti][:], in_=ps[:, b, 0:TT], func=AF.Silu),
                                    reads=[("ps", bG)], writes=[("tmpA", ti)])
                                a = fc + fi - f0
                                P.add("dve", lambda e, b=bU, ti=ti, a=a, tt=tt: e.tensor_tensor(
                                    out=act[:, a, tl(tt)], in0=ps[:, b, 0:TT], in1=tmpA[ti][:], op=ALU.mult),
                                    reads=[("ps", bU), ("tmpA", ti)], writes=[("act", a, tt)])
                    add_block([(wg[:, :, fc * 128:(fc + nb) * 128], (KC, nb * 128)),
                               (wu[:, :, fc * 128:(fc + nb) * 128], (KC, nb * 128))], compute)
                    fc += nb
                if f1 == FC:
                    ple_p_load(L)
                proj_residual(ffn_w_down[L][f0 * 128:f1 * 128, :], f1 - f0, after if f1 == FC else None)

        def ple(L, after):
            pv = pT[L].rearrange("(kc p) t -> p kc t", p=128)
            wpv = ple_w_proj[L].rearrange("(kc p) n -> p kc n", p=128)
            wgv = wview(ple_w_gate[L])
            shared = {}

            def tile_compute(tt, pblk, kp):
                gate = shared["gate"]
                wpp, kpp = shared["wpp"]
                for dc in range(KC):
                    blk, bkey = gate[dc // 2]
                    di = dc % 2
                    g = cnt["g"] % 3
                    cnt["g"] += 1
                    bG, bP = g, 3 + g

                    def mm(e, blk=blk, di=di, b=bG):
                        for kc in range(KC):
                            i = e.matmul(ps[:, b, 0:TT], blk[:, kc, di * 128:(di + 1) * 128],
                                         hn[:, kc, tl(tt)], start=(kc == 0), stop=(kc == KC - 1))
                        return i
                    P.add("pe", mm, reads=[bkey] + [("hn", kc, tt) for kc in range(KC)], writes=[("ps", bG)])

                    def mmp(e, dc=dc, b=bP):
                        for k2 in range(2):
                            i = e.matmul(ps[:, b, 0:TT], wpp[:, k2, dc * 128:(dc + 1) * 128],
                                         pblk[:, k2, :], start=(k2 == 0), stop=(k2 == 1))
                        return i
                    P.add("pe", mmp, reads=[kpp, kp] + [("act", 6, tt), ("act", 7, tt)], writes=[("ps", bP)])
                    ti = cnt["tmp"] % 2
                    cnt["tmp"] += 1
                    P.add("act", lambda e, b=bG, ti=ti: e.activation(
                        out=tmpA[ti][:], in_=ps[:, b, 0:TT], func=AF.Sigmoid),
                        reads=[("ps", bG)], writes=[("tmpA", ti)])
                    P.add("dve", lambda e, b=bP, ti=ti: e.tensor_tensor(
                        out=ps[:, b, 0:TT], in0=ps[:, b, 0:TT], in1=tmpA[ti][:], op=ALU.mult),
                        reads=[("ps", bP), ("tmpA", ti)], writes=[("ps", bP)])
                    P.add("dve", lambda e, dc=dc, b=bP: e.tensor_tensor(
                        out=h[:, dc, tl(tt)], in0=ps[:, b, 0:TT], in1=h[:, dc, tl(tt)], op=ALU.add),
                        reads=[("ps", bP), ("h", dc, tt)], writes=[("h", dc, tt)])
                    interleave(after, tt, dc)

            def main(views):
                shared["gate"] = views[0:4]
                shared["wpp"] = views[4]
                tile_compute(0, act[:, 6:8, tl(0)], "p")
            add_block([(wgv[:, :, dp * 256:(dp + 1) * 256], (KC, 256)) for dp in range(4)] +
                      [(wpv, (2, D))], main, hold=NT - 1)
            for tt in range(1, NT):
                add_block([], lambda views, tt=tt: tile_compute(tt, act[:, 6:8, tl(tt)], "p"))

        def ple_p_load(L):
            pv = pT[L].rearrange("(kc p) t -> p kc t", p=128)

            def compute(views):
                for k2 in range(2):
                    load_to(pv[:, k2:k2 + 1, :].rearrange("p a (c t) -> p (a c) t", t=min(T, STG_EL)),
                            (T // min(T, STG_EL), min(T, STG_EL)), act[:, 6 + k2, :],
                            ["p"] + [("act", 6 + k2, tt) for tt in range(NT)])
            add_block([], compute)

        norm_block(Norm(C_MIX + layers[0] * 8))
        if layers[0] % 2 == 1:
            sg_setup_early(layers[0] // 2)
        for li, L in enumerate(layers):
            j = L // 2
            n_ffn = Norm(C_FFN + L * 8)
            n_ple = Norm(C_PLE + L * 8)
            if li + 1 < len(layers):
                n_next = Norm(C_MIX + layers[li + 1] * 8)
            elif final:
                n_next = Norm(C_FIN, to_out=True)
            else:
                n_next = None
            if L % 2 == 0:
                conv_mixer(j, n_ffn)
            else:
                sg_mixer(j, n_ffn)
            ffn(L, n_ple)
            if li + 1 < len(layers) and layers[li + 1] % 2 == 1:
                sg_setup_early(layers[li + 1] // 2)
            ple(L, n_next)
        if not final:
            def dump(views):
                for kc in range(KC):
                    P.add("sp", lambda e, kc=kc: e.dma_start(out=outT[kc * 128:(kc + 1) * 128, :], in_=h[:, kc, :]),
                          reads=[("h", kc, tt) for tt in range(NT)], writes=["out"], dma_key="outh%d" % kc)
            add_block([], dump)

        from collections import deque
        free = deque(range(NSLOT))
        release_at = {}
        nblk = len(blocks)
        flat = [(bi, li) for bi in range(nblk) for li in range(len(blocks[bi]["loads"]))]
        for b in blocks:
            b["views"] = [None] * len(b["loads"])
            b["slots"] = []
        nxt = 0

        def issue_one():
            nonlocal nxt
            bi, li = flat[nxt]
            b = blocks[bi]
            si = free.popleft()
            ap, shp = b["loads"][li]
            b["views"][li] = load_block(ap, shp, si)
            release_at.setdefault(bi + b["hold"], []).append(si)
            nxt += 1

        assert not blocks[0]["loads"]
        if flat:
            first_blk = flat[0][0]
            while nxt < len(flat) and flat[nxt][0] == first_blk:
                issue_one()
        if NT == 4:
            load_x(1)
            P.add("sp", lambda e: e.dma_start(out=h[:, :, 2 * TT:4 * TT], in_=xv[:, :, 2 * TT:4 * TT]),
                  writes=[("h", kc, tt) for kc in range(KC) for tt in (2, 3)], dma_key="x23")
        else:
            for tt in range(1, NT):
                load_x(tt)
        blocks[0]["compute"]([])
        for i in range(1, nblk):
            while nxt < len(flat) and flat[nxt][0] <= i:
                assert free, "not enough weight slots"
                issue_one()
            blocks[i]["compute"](blocks[i]["views"])
            for s in release_at.pop(i, []):
                free.append(s)
            while nxt < len(flat) and free:
                issue_one()

        P.add("sp", lambda e: e.nop(), reads=["out"] + [("out", kc, tt) for kc in range(KC) for tt in range(NT)],
              writes=["done"])

        with nc.Block() as block:
            P.finalize(block)
        P.close()
        PROG_STATS["ops"] = (len(P.ops), dict(P.eng_count), dict(P.sig_counts))
    return nc


def fm_cols(v):
    v = np.asarray(v, dtype=np.float32).reshape(-1, KC, 128)
    return np.ascontiguousarray(v.transpose(2, 0, 1).reshape(128, -1))


def make_in_maps(inp, T=SEQ, n_cores=N_CORES):
    f = lambda a: np.ascontiguousarray(np.asarray(a, dtype=np.float32))
    smalls = np.concatenate([
        fm_cols(inp["mix_norm"]), fm_cols(inp["ffn_norm"]), fm_cols(inp["ple_norm"]),
        fm_cols(inp["final_norm"]), fm_cols(np.asarray(inp["conv_w"]).reshape(-1, D)),
        fm_cols(inp["sg_v_bias"]), fm_cols(inp["sg_v_gain"])], axis=1)
    assert smalls.shape == (128, NS), smalls.shape
    shared = {
        "smalls": f(smalls),
        "conv_w_in": f(inp["conv_w_in"]), "conv_w_out": f(inp["conv_w_out"]),
        "sg_w_in": f(inp["sg_w_in"]), "sg_w_out": f(inp["sg_w_out"]),
        "sg_bs": f(np.asarray(inp["sg_b_spatial"]).reshape(2, D)),
        "sg_wsT": f(np.asarray(inp["sg_w_spatial"]).transpose(0, 3, 1, 2).reshape(2, 128, D)),
        "ffn_w_gate": f(inp["ffn_w_gate"]), "ffn_w_up": f(inp["ffn_w_up"]),
        "ffn_w_down": f(inp["ffn_w_down"]),
        "ple_w_gate": f(inp["ple_w_gate"]), "ple_w_proj": f(inp["ple_w_proj"]),
    }
    x = np.asarray(inp["x"], dtype=np.float32)
    p = np.asarray(inp["p"], dtype=np.float32)
    maps = []
    for c in range(n_cores):
        m = dict(shared)
        m["xT"] = np.ascontiguousarray(x[c].T)
        m["pT"] = np.ascontiguousarray(p[:, c].transpose(0, 2, 1))
        maps.append(m)
    return maps


_NC_CACHE = {}


def kernel(**inputs):
    x = np.asarray(inputs["x"])
    B, T, _ = x.shape
    assert B == N_CORES and T == SEQ
    if "nc" not in _NC_CACHE:
        _NC_CACHE["nc"] = build_program(T)
    nc = _NC_CACHE["nc"]
    in_maps = make_in_maps(inputs, T, N_CORES)
    res = run_bass_kernel_spmd(nc, in_maps, core_ids=list(range(N_CORES)))
    out = np.stack([np.asarray(r["outT"]).T for r in res.results], axis=0)
    return np.ascontiguousarray(out.astype(np.float32))
```

```python
from contextlib import ExitStack

import numpy as np
import concourse.bass as bass
import concourse.mybir as mybir
from concourse.bass_utils import run_bass_kernel_spmd

F32 = mybir.dt.float32
BF16 = mybir.dt.bfloat16
ALU = mybir.AluOpType
AF = mybir.ActivationFunctionType

D = 1024
KC = 8
DFF = 2816
FC = 22
PLE = 256
DEPTH = 4
RMS_EPS = 1e-6
LN_EPS = 1e-5
N_CORES = 8
SEQ = 2048

C_MIX = 0
C_FFN = 32
C_PLE = 64
C_FIN = 96
C_CONVW = 104
C_VBIAS = 152
C_VGAIN = 168
NS = 184

ENGS = ("pe", "act", "dve", "pool", "sp")
PROG_STATS = {}


class Op:
    __slots__ = ("eng", "emit", "deps", "signal", "seq", "dma_sem", "dma_val", "idx", "eidx")

    def __init__(self, eng, emit):
        self.eng = eng
        self.emit = emit
        self.deps = []
        self.signal = False
        self.seq = 0
        self.dma_sem = None
        self.dma_val = 0
        self.idx = 0
        self.eidx = 0


class Prog:
    NEAR = 10 ** 9

    def __init__(self, nc):
        self.nc = nc
        self.ops = []
        self.last_w = {}
        self.readers = {}
        self.eng_count = {e: 0 for e in ENGS}
        self.dma_sems = {}
        self._sem_ctx = []

    def _new_sem(self, name):
        cm = self.nc.semaphore(name)
        s = cm.__enter__()
        self._sem_ctx.append(cm)
        return s

    def add(self, eng, emit, reads=(), writes=(), dma_key=None):
        op = Op(eng, emit)
        op.idx = len(self.ops)
        op.eidx = self.eng_count[eng]
        self.eng_count[eng] += 1
        deps = {}

        def dep(o):
            if o is None or o is op:
                return
            deps[o.idx] = o

        for r in reads:
            dep(self.last_w.get(r))
        for w in writes:
            dep(self.last_w.get(w))
            for rd in self.readers.get(w, ()):
                dep(rd)
        for w in writes:
            self.last_w[w] = op
            self.readers[w] = []
        for r in reads:
            if r in writes:
                continue
            lst = self.readers.setdefault(r, [])
            if dma_key is None:
                lst[:] = [o for o in lst if not (o.eng == eng and o.dma_sem is None)]
            lst.append(op)
        if dma_key is not None:
            ent = self.dma_sems.get(dma_key)
            if ent is None:
                ent = [self._new_sem("dq_%s" % dma_key), 0]
                self.dma_sems[dma_key] = ent
            ent[1] += 16
            op.dma_sem = ent[0]
            op.dma_val = ent[1]
        op.deps = list(deps.values())
        self.ops.append(op)
        return op

    def _skip_same_engine(self, op, d):
        if d.eng != op.eng or d.dma_sem is not None:
            return False
        if d.eng == "pe":
            return True
        return op.eidx - d.eidx > self.NEAR

    def finalize(self, block):
        for op in self.ops:
            for d in op.deps:
                if d.dma_sem is None and not self._skip_same_engine(op, d):
                    d.signal = True
        esem = {e: self._new_sem("es_" + e) for e in ENGS}
        cnt = {e: 0 for e in ENGS}
        for op in self.ops:
            if op.dma_sem is None and op.signal:
                cnt[op.eng] += 1
                op.seq = cnt[op.eng]
        self.sig_counts = dict(cnt)
        per_eng = {e: [o for o in self.ops if o.eng == e] for e in ENGS}

        def run(engname, eh):
            known = {}
            for op in per_eng[engname]:
                need = {}
                for d in op.deps:
                    if d.dma_sem is not None:
                        key = ("d", id(d.dma_sem))
                        sem, val = d.dma_sem, d.dma_val
                    else:
                        if self._skip_same_engine(op, d):
                            continue
                        key = ("e", d.eng)
                        sem, val = esem[d.eng], d.seq
                    if known.get(key, 0) >= val:
                        continue
                    if key not in need or need[key][1] < val:
                        need[key] = (sem, val)
                for key, (sem, val) in need.items():
                    eh.wait_ge(sem, val)
                    known[key] = val
                inst = op.emit(eh)
                if op.dma_sem is not None:
                    inst.then_inc(op.dma_sem, 16)
                elif op.signal:
                    inst.then_inc(esem[engname], 1)

        @block.tensor
        def _(e):
            run("pe", e)

        @block.scalar
        def _(e):
            run("act", e)

        @block.vector
        def _(e):
            run("dve", e)

        @block.gpsimd
        def _(e):
            run("pool", e)

        @block.sync
        def _(e):
            run("sp", e)

    def close(self):
        for cm in reversed(self._sem_ctx):
            cm.__exit__(None, None, None)
        self._sem_ctx = []


def build_program(T=SEQ, layers=(0, 1, 2, 3), final=True):
    TT = min(512, T)
    NT = T // TT
    NCH = T // 128
    nc = bass.Bass("TRN2", target_bir_lowering=False)

    def din(name, shape):
        return nc.dram_tensor(name, list(shape), F32, kind="ExternalInput").ap()

    xT = din("xT", [D, T])
    pT = din("pT", [DEPTH, PLE, T])
    smalls_d = din("smalls", [128, NS])
    conv_w_in = din("conv_w_in", [2, D, 3 * D])
    conv_w_out = din("conv_w_out", [2, D, D])
    sg_w_in = din("sg_w_in", [2, D, 2 * D])
    sg_w_out = din("sg_w_out", [2, D, D])
    sg_bs = din("sg_bs", [2, D])
    sg_wsT = din("sg_wsT", [2, 128, D])
    ffn_w_gate = din("ffn_w_gate", [DEPTH, D, DFF])
    ffn_w_up = din("ffn_w_up", [DEPTH, D, DFF])
    ffn_w_down = din("ffn_w_down", [DEPTH, DFF, D])
    ple_w_gate = din("ple_w_gate", [DEPTH, D, D])
    ple_w_proj = din("ple_w_proj", [DEPTH, PLE, D])
    outT = nc.dram_tensor("outT", [D, T], F32, kind="ExternalOutput").ap()

    NSLOT = 10
    NSTG = 3
    SLOT_EL = 2048
    STG_EL = 1024

    with ExitStack() as es:
        def sb(name, shape, dt):
            return es.enter_context(nc.sbuf_tensor(name, list(shape), dt))

        h = sb("h", [128, KC, T], F32)
        hn = sb("hn", [128, KC, T], BF16)
        act = sb("act", [128, KC, T], BF16)
        slots = [sb("slot%d" % i, [128, SLOT_EL], BF16) for i in range(NSLOT)]
        stgs = [sb("stg%d" % i, [128, STG_EL], F32) for i in range(NSTG)]
        smalls = sb("smalls_sb", [128, NS], F32)
        onesD = sb("onesD", [128, 128], BF16)
        ones1 = sb("ones1", [128, 128], BF16)
        rstd_t = [sb("rstd%d" % i, [128, TT], F32) for i in range(1)]
        NSQ = 8
        tmpA = [sb("tmpA%d" % i, [128, TT], F32) for i in range(2)]
        tmpB = [sb("tmpB%d" % i, [128, TT], F32) for i in range(2)]
        zbuf = sb("zbuf", [128, 2 + 2048], F32)
        vhat_all = zbuf[:, 0:D].bitcast(BF16)
        vhat_b = [vhat_all[:, 0:D], vhat_all[:, D:2 * D]]
        sq_all = zbuf[:, 0:2048].bitcast(BF16)
        sq_t = [sq_all[:, i * 512:i * 512 + TT] for i in range(NSQ)]
        SCR_KEYS = (["zpad", ("vhat", 0), ("vhat", 1)] + [("z", tt) for tt in range(NT)] +
                    [("sq", i) for i in range(NSQ)])
        ACT_KEYS = [("act", j, tt) for j in range(KC) for tt in range(NT)]
        Rt = sb("Rt", [128, D], F32)
        wsT = sb("wsT", [128, D], BF16)
        stats = sb("stats", [128, 2, 2, 6], F32)
        mv = sb("mv", [128, 2, 2], F32)
        lnr = sb("lnr", [128, 2, 4], F32)
        epsr = sb("epsr", [128, 2], F32)
        ps = es.enter_context(nc.psum_tensor("ps", [128, 8, 512], F32))

        P = Prog(nc)
        blocks = []

        def add_block(loads, compute, hold=0):
            blocks.append({"loads": loads, "compute": compute, "hold": hold})

        def tl(tt):
            return slice(tt * TT, (tt + 1) * TT)

        P.add("pool", lambda e: e.memset(onesD[:], 1.0 / D), writes=["onesD"])
        P.add("pool", lambda e: e.memset(ones1[:], 1.0), writes=["ones1"])
        P.add("pool", lambda e: e.memset(epsr[:, 0:1], RMS_EPS), writes=["eps"])
        P.add("pool", lambda e: e.memset(epsr[:, 1:2], LN_EPS), writes=["eps"])
        P.add("sp", lambda e: e.dma_start(out=smalls[:], in_=smalls_d), writes=["smalls"], dma_key="sm")
        xv = xT.rearrange("(kc p) t -> p kc t", p=128)

        def load_x(tt):
            P.add("sp", lambda e, tt=tt: e.dma_start(out=h[:, :, tl(tt)], in_=xv[:, :, tl(tt)]),
                  writes=[("h", kc, tt) for kc in range(KC)], dma_key="x%d" % tt)
        for hh in range(2):
            P.add("sp", lambda e, hh=hh: e.dma_start(out=h[:, 4 * hh:4 * hh + 4, tl(0)],
                                                     in_=xv[:, 4 * hh:4 * hh + 4, tl(0)]),
                  writes=[("h", kc, 0) for kc in range(4 * hh, 4 * hh + 4)], dma_key="x0h%d" % hh)

        state = {"slot": 0, "stg": 0, "cast": 0}
        cast_engs = ("pool",)

        def load_to(src_ap, shape, dst2d, wkeys):
            a, b = shape
            assert b <= STG_EL
            rows = max(1, STG_EL // b)
            a0 = 0
            while a0 < a:
                a1 = min(a, a0 + rows)
                m = (a1 - a0) * b
                gi = state["stg"] % NSTG
                state["stg"] += 1
                stg_v = stgs[gi][:, 0:m].rearrange("p (a b) -> p a b", a=a1 - a0)
                P.add("sp", lambda e, stg_v=stg_v, a0=a0, a1=a1: e.dma_start(out=stg_v, in_=src_ap[:, a0:a1, :]),
                      writes=[("stg", gi)], dma_key="stg%d" % gi)
                dst = dst2d[:, a0 * b:a0 * b + m]
                srcv = stgs[gi][:, 0:m]
                P.add("pool", lambda e, dst=dst, srcv=srcv: e.tensor_copy(out=dst, in_=srcv),
                      reads=[("stg", gi)], writes=wkeys)
                a0 = a1

        def load_block(src_ap, shape, si):
            a, b = shape
            n = a * b
            assert n <= SLOT_EL and b <= STG_EL
            rows = max(1, STG_EL // b)
            a0 = 0
            while a0 < a:
                a1 = min(a, a0 + rows)
                m = (a1 - a0) * b
                gi = state["stg"] % NSTG
                state["stg"] += 1
                ce = cast_engs[state["cast"] % len(cast_engs)]
                state["cast"] += 1
                stg_v = stgs[gi][:, 0:m].rearrange("p (a b) -> p a b", a=a1 - a0)
                P.add("sp", lambda e, stg_v=stg_v, a0=a0, a1=a1: e.dma_start(out=stg_v, in_=src_ap[:, a0:a1, :]),
                      writes=[("stg", gi)], dma_key="stg%d" % gi)
                dst = slots[si][:, a0 * b:a0 * b + m]
                srcv = stgs[gi][:, 0:m]
                if ce == "act":
                    P.add("act", lambda e, dst=dst, srcv=srcv: e.activation(out=dst, in_=srcv, func=AF.Copy),
                          reads=[("stg", gi)], writes=[("slot", si)])
                else:
                    P.add(ce, lambda e, dst=dst, srcv=srcv: e.tensor_copy(out=dst, in_=srcv),
                          reads=[("stg", gi)], writes=[("slot", si)])
                a0 = a1
            slot_v = slots[si][:, 0:n].rearrange("p (a b) -> p a b", a=a)
            return slot_v, ("slot", si)

        def wview(w2d):
            return w2d.rearrange("(kc p) n -> p kc n", p=128)

        cnt = {"sq": 0, "rstd": 0, "hn": 0, "tmp": 0, "g": 0, "out": 0}
        otiles = []
        per = (T * 2) // (4 * TT)
        for jj in range(6):
            a32 = act[:, jj, :].bitcast(F32)
            for hh in range(per):
                tts = sorted(set(((hh * TT * 2 + o) // TT) for o in range(0, 2 * TT, TT)) & set(range(NT)))
                otiles.append((a32[:, hh * TT:(hh + 1) * TT], [("act", jj, t_) for t_ in tts]))

        class Norm:
            def __init__(self, gcol0, to_out=False):
                self.gcol0 = gcol0
                self.to_out = to_out
                self.ri = {}

            def square_one(self, tt, kc):
                wr = [("sq", kc)] + (SCR_KEYS if kc == 0 else [])
                P.add("act", lambda e: e.activation(out=sq_t[kc], in_=h[:, kc, tl(tt)], func=AF.Square),
                      reads=[("h", kc, tt)], writes=wr)

            def squares(self, tt):
                for kc in range(KC):
                    self.square_one(tt, kc)

            def stats(self, tt):
                for kc in range(KC):
                    P.add("pe", lambda e, kc=kc: e.matmul(
                        ps[:, 7, 0:TT], onesD[:], sq_t[kc], start=(kc == 0), stop=(kc == KC - 1)),
                        reads=[("sq", kc), "onesD"], writes=[("ps", 7)])
                ri = 0
                cnt["rstd"] += 1
                self.ri[tt] = ri
                P.add("act", lambda e, ri=ri: e.activation(
                    out=rstd_t[ri][:], in_=ps[:, 7, 0:TT], func=AF.Ln, bias=epsr[:, 0:1]),
                    reads=[("ps", 7), "eps"], writes=[("rstd", ri)])
                P.add("act", lambda e, ri=ri: e.activation(
                    out=ps[:, 6, 0:TT], in_=rstd_t[ri][:], func=AF.Exp, scale=-0.5),
                    reads=[("rstd", ri)], writes=[("ps", 6)])

            def apply(self, tt, kcs):
                ri = self.ri[tt]
                gcol0 = self.gcol0
                for kc in kcs:
                    if not self.to_out:
                        P.add("dve", lambda e, kc=kc, tt=tt, ri=ri: e.scalar_tensor_tensor(
                            out=hn[:, kc, tl(tt)], in0=h[:, kc, tl(tt)],
                            scalar=smalls[:, gcol0 + kc:gcol0 + kc + 1], in1=ps[:, 6, 0:TT],
                            op0=ALU.mult, op1=ALU.mult),
                            reads=[("h", kc, tt), ("ps", 6), "smalls"], writes=[("hn", kc, tt)])
                    else:
                        oi = cnt["out"] % len(otiles)
                        cnt["out"] += 1
                        oap, okeys = otiles[oi]
                        P.add("dve", lambda e, kc=kc, tt=tt, oap=oap: e.scalar_tensor_tensor(
                            out=oap, in0=h[:, kc, tl(tt)],
                            scalar=smalls[:, gcol0 + kc:gcol0 + kc + 1], in1=ps[:, 6, 0:TT],
                            op0=ALU.mult, op1=ALU.mult),
                            reads=[("h", kc, tt), ("ps", 6), "smalls"], writes=[("otile", oi)] + okeys)
                        P.add("sp", lambda e, kc=kc, tt=tt, oap=oap: e.dma_start(
                            out=outT[kc * 128:(kc + 1) * 128, tl(tt)], in_=oap),
                            reads=[("otile", oi)], writes=[("out", kc, tt)], dma_key="out%d" % oi)

            def whole_tile(self, tt):
                self.squares(tt)
                self.stats(tt)
                self.apply(tt, range(KC))

        def norm_block(norm):
            def compute(views):
                for tt in range(NT):
                    norm.whole_tile(tt)
            add_block([], compute)

        APPLY_AT = {3: (0,), 4: (1, 2), 5: (3, 4), 6: (5, 6), 7: (7,)}
        SQ_LAG = 2

        def interleave(after, tt, dc):
            if after is None:
                return
            n = tt * KC + dc
            m = n - SQ_LAG
            if m >= 0:
                after.square_one(m // KC, m % KC)
            if tt >= 1:
                if dc == SQ_LAG - 1:
                    after.stats(tt - 1)
                elif dc in APPLY_AT:
                    after.apply(tt - 1, APPLY_AT[dc])
            if dc == KC - 1 and tt == NT - 1:
                for m in range(n - SQ_LAG + 1, n + 1):
                    after.square_one(m // KC, m % KC)
                after.stats(tt)
                after.apply(tt, range(KC))

        def proj_residual(w2d, nk, after=None):
            wv = wview(w2d)

            def compute(views):
                for tt in range(NT):
                    for dc in range(KC):
                        blk, bkey = views[dc // 2]
                        di = dc % 2
                        b = 4 + (cnt["g"] % 2)
                        cnt["g"] += 1

                        def mm(e, blk=blk, di=di, tt=tt, b=b):
                            for j in range(nk):
                                i = e.matmul(ps[:, b, 0:TT], blk[:, j, di * 128:(di + 1) * 128],
                                             act[:, j, tl(tt)], start=(j == 0), stop=(j == nk - 1))
                            return i
                        P.add("pe", mm, reads=[bkey] + [("act", j, tt) for j in range(nk)],
                              writes=[("ps", b)])
                        P.add("dve", lambda e, dc=dc, tt=tt, b=b: e.tensor_tensor(
                            out=h[:, dc, tl(tt)], in0=ps[:, b, 0:TT], in1=h[:, dc, tl(tt)], op=ALU.add),
                            reads=[("ps", b), ("h", dc, tt)], writes=[("h", dc, tt)])
                        interleave(after, tt, dc)
            add_block([(wv[:, :, dp * 256:(dp + 1) * 256], (nk, 256)) for dp in range(4)], compute)

        def conv_mixer(j, after):
            wv = wview(conv_w_in[j])
            wc0 = C_CONVW + j * 24

            def setup(views):
                P.add("pool", lambda e: e.memset(zbuf[:, 0:2], 0.0), writes=SCR_KEYS)
            add_block([], setup)
            for cp in range(4):
                def compute(views, cp=cp):
                    (blkC, kC_), (blkX, kX_), (blkB, kB_) = views
                    for ci in range(2):
                        cc = cp * 2 + ci
                        for tt in range(NT):
                            g = cnt["g"] % 2
                            cnt["g"] += 1
                            bC, bX, bB = 3 * g, 3 * g + 1, 3 * g + 2

                            def mm(e, blk, b, ci=ci, tt=tt):
                                for kc in range(KC):
                                    i = e.matmul(ps[:, b, 0:TT], blk[:, kc, ci * 128:(ci + 1) * 128],
                                                 hn[:, kc, tl(tt)], start=(kc == 0), stop=(kc == KC - 1))
                                return i
                            hreads = [("hn", kc, tt) for kc in range(KC)]
                            P.add("pe", lambda e, blk=blkC, b=bC, mm=mm: mm(e, blk, b), reads=[kC_] + hreads,
                                  writes=[("ps", bC)])
                            P.add("pe", lambda e, blk=blkX, b=bX, mm=mm: mm(e, blk, b), reads=[kX_] + hreads,
                                  writes=[("ps", bX)])
                            P.add("pe", lambda e, blk=blkB, b=bB, mm=mm: mm(e, blk, b), reads=[kB_] + hreads,
                                  writes=[("ps", bB)])
                            ti = cnt["tmp"] % 2
                            cnt["tmp"] += 1
                            P.add("act", lambda e, b=bC, ti=ti: e.activation(
                                out=tmpA[ti][:], in_=ps[:, b, 0:TT], func=AF.Copy),
                                reads=[("ps", bC)], writes=[("tmpA", ti)])
                            t0 = tt * TT
                            P.add("dve", lambda e, b=bX, ti=ti, t0=t0: e.tensor_tensor(
                                out=zbuf[:, 2 + t0:2 + t0 + TT], in0=ps[:, b, 0:TT], in1=tmpA[ti][:],
                                op=ALU.mult),
                                reads=[("ps", bX), ("tmpA", ti)], writes=[("z", tt)])
                            zr = [("z", tt), "zpad"] + ([("z", tt - 1)] if tt > 0 else [])
                            c0 = wc0 + 0 * 8 + cc
                            c1 = wc0 + 1 * 8 + cc
                            c2 = wc0 + 2 * 8 + cc
                            P.add("act", lambda e, b=bC, t0=t0, c0=c0: e.activation(
                                out=ps[:, b, 0:TT], in_=zbuf[:, t0:t0 + TT], func=AF.Copy,
                                scale=smalls[:, c0:c0 + 1]),
                                reads=zr + ["smalls"], writes=[("ps", bC)])
                            P.add("dve", lambda e, b=bC, t0=t0, c1=c1: e.scalar_tensor_tensor(
                                out=ps[:, b, 0:TT], in0=zbuf[:, t0 + 1:t0 + 1 + TT], scalar=smalls[:, c1:c1 + 1],
                                in1=ps[:, b, 0:TT], op0=ALU.mult, op1=ALU.add),
                                reads=zr + ["smalls", ("ps", bC)], writes=[("ps", bC)])
                            P.add("dve", lambda e, b=bC, ti=ti, t0=t0, c2=c2: e.scalar_tensor_tensor(
                                out=tmpB[ti][:], in0=zbuf[:, t0 + 2:t0 + 2 + TT], scalar=smalls[:, c2:c2 + 1],
                                in1=ps[:, b, 0:TT], op0=ALU.mult, op1=ALU.add),
                                reads=zr + ["smalls", ("ps", bC)], writes=[("tmpB", ti)])
                            P.add("dve", lambda e, b=bB, ti=ti, cc=cc, tt=tt: e.tensor_tensor(
                                out=act[:, cc, tl(tt)], in0=ps[:, b, 0:TT], in1=tmpB[ti][:], op=ALU.mult),
                                reads=[("ps", bB), ("tmpB", ti)], writes=[("act", cc, tt)])
                add_block([(wv[:, :, D + cp * 256:D + (cp + 1) * 256], (KC, 256)),
                           (wv[:, :, 2 * D + cp * 256:2 * D + (cp + 1) * 256], (KC, 256)),
                           (wv[:, :, cp * 256:(cp + 1) * 256], (KC, 256))], compute)
            proj_residual(conv_w_out[j], KC, after)

        def sg_setup_early(j):
            def setup(views):
                P.add("sp", lambda e: e.dma_start(out=Rt[:], in_=sg_bs[j:j + 1, :].broadcast_to([128, D])),
                      writes=["Rt"], dma_key="rt")
                gi = state["stg"] % NSTG
                state["stg"] += 1
                sv = stgs[gi][:, 0:D].rearrange("p (g t) -> p g t", g=8)
                P.add("sp", lambda e: e.dma_start(out=stgs[gi][:, 0:D], in_=sg_wsT[j]),
                      writes=[("stg", gi)], dma_key="stg%d" % gi)
                P.add("pool", lambda e: e.affine_select(
                    out=sv, in_=sv, pattern=[[0, 8], [1, 128]], compare_op=ALU.is_ge, fill=0.0,
                    base=0, channel_multiplier=-1),
                    reads=[("stg", gi)], writes=[("stg", gi)])
                P.add("pool", lambda e: e.tensor_copy(out=wsT[:], in_=stgs[gi][:, 0:D]),
                      reads=[("stg", gi)], writes=["wsT"])
            add_block([], setup)

        def sg_mixer(j, after):
            wv = wview(sg_w_in[j])
            vg0 = C_VGAIN + j * 8
            vb0 = C_VBIAS + j * 8
            deferred = []

            def handover(views):
                P.add("pool", lambda e: e.memset(zbuf[:, 2 * D:2 * D + 2], 0.0), writes=SCR_KEYS)
                for half in range(2):
                    P.add("pe", lambda e, half=half: e.matmul(
                        ps[:, 4 + half, :], ones1[:], wsT[:, half * 512:(half + 1) * 512],
                        start=True, stop=True),
                        reads=["ones1", "wsT"], writes=[("ps", 4 + half)])
                for g in range(8):
                    b = 4 + g // 4
                    o = (g % 4) * 128
                    P.add("dve", lambda e, g=g, b=b, o=o: e.scalar_tensor_tensor(
                        out=Rt[:, g * 128:(g + 1) * 128], in0=ps[:, b, o:o + 128],
                        scalar=smalls[:, vb0 + g:vb0 + g + 1], in1=Rt[:, g * 128:(g + 1) * 128],
                        op0=ALU.mult, op1=ALU.add),
                        reads=[("ps", b), "smalls", "Rt"], writes=["Rt"])
            add_block([], handover)

            def vstage(views):
                vblk = {(0, 0): views[0], (0, 1): views[1], (1, 0): views[2], (1, 1): views[3]}

                def part1(tc):
                    pr = tc % 2
                    p3 = tc % 3
                    tt = (tc * 128) // TT
                    csl = slice(tc * 128, (tc + 1) * 128)
                    for cb in range(2):
                        b = 2 * p3 + cb

                        def mmv(e, cb=cb, b=b, csl=csl):
                            for kc in range(KC):
                                blk = vblk[(cb, kc // 4)][0]
                                i = e.matmul(ps[:, b, :], hn[:, kc, csl], blk[:, kc % 4, :],
                                             start=(kc == 0), stop=(kc == KC - 1))
                            return i
                        P.add("pe", mmv, reads=[vblk[(cb, 0)][1], vblk[(cb, 1)][1]] +
                              [("hn", kc, tt) for kc in range(KC)], writes=[("ps", b)])

                def part1b(tc):
                    pr = tc % 2
                    p3 = tc % 3
                    pv = ps[:, 2 * p3:2 * p3 + 2, :]
                    pkeys = [("ps", 2 * p3), ("ps", 2 * p3 + 1)]
                    for cb in range(2):
                        P.add("dve", lambda e, cb=cb, pr=pr, p3=p3: e.bn_stats(
                            out=stats[:, pr, cb, :], in_=ps[:, 2 * p3 + cb, :]),
                            reads=[pkeys[cb]], writes=[("stats", pr, cb)])
                    P.add("dve", lambda e, pr=pr: e.bn_aggr(
                        out=mv[:, pr, :], in_=stats[:, pr, :, :].rearrange("p a b -> p (a b)")),
                        reads=[("stats", pr, 0), ("stats", pr, 1)], writes=[("mv", pr)])
                    P.add("act", lambda e, pr=pr: e.activation(
                        out=lnr[:, pr, 0:1], in_=mv[:, pr, 1:2], func=AF.Ln, bias=epsr[:, 1:2]),
                        reads=[("mv", pr), "eps"], writes=[("lnr0", pr)])
                    P.add("act", lambda e, pr=pr: e.activation(
                        out=lnr[:, pr, 1:2], in_=lnr[:, pr, 0:1], func=AF.Exp, scale=-0.5),
                        reads=[("lnr0", pr)], writes=[("lnr1", pr)])

                def part1c(tc):
                    pr = tc % 2
                    p3 = tc % 3
                    pv = ps[:, 2 * p3:2 * p3 + 2, :]
                    pkeys = [("ps", 2 * p3), ("ps", 2 * p3 + 1)]
                    P.add("dve", lambda e, pr=pr: e.scalar_tensor_tensor(
                        out=lnr[:, pr, 2:3], in0=mv[:, pr, 0:1], scalar=-1.0, in1=lnr[:, pr, 1:2],
                        op0=ALU.mult, op1=ALU.mult),
                        reads=[("mv", pr), ("lnr1", pr)], writes=[("lnr2", pr)])
                    P.add("act", lambda e, pv=pv, pr=pr: e.activation(
                        out=vhat_b[pr].rearrange("p (a b) -> p a b", a=2), in_=pv, func=AF.Identity,
                        bias=lnr[:, pr, 2:3], scale=lnr[:, pr, 1:2]),
                        reads=pkeys + [("lnr1", pr), ("lnr2", pr)], writes=[("vhat", pr)])

                def part2(tc, mid=None):
                    pr = tc % 2
                    tt = (tc * 128) // TT
                    csl = slice(tc * 128, (tc + 1) * 128)
                    mb = 6

                    def mmg(e, pr=pr, mb=mb):
                        for g in range(8):
                            i = e.matmul(ps[:, mb + g // 4, (g % 4) * 128:(g % 4 + 1) * 128],
                                         vhat_b[pr][:, g * 128:(g + 1) * 128], wsT[:, g * 128:(g + 1) * 128],
                                         start=True, stop=True)
                        return i
                    P.add("pe", mmg, reads=[("vhat", pr), "wsT"], writes=[("ps", mb), ("ps", mb + 1)])
                    if mid is not None:
                        mid()
                    for g in range(8):
                        P.add("dve", lambda e, g=g, mb=mb, csl=csl: e.scalar_tensor_tensor(
                            out=act[:, g, csl], in0=ps[:, mb + g // 4, (g % 4) * 128:(g % 4 + 1) * 128],
                            scalar=smalls[:, vg0 + g:vg0 + g + 1], in1=Rt[:, g * 128:(g + 1) * 128],
                            op0=ALU.mult, op1=ALU.add),
                            reads=[("ps", mb + g // 4), "Rt", "smalls"], writes=[("act", g, tt)])

                for tc in range(NCH + 2):
                    if tc < NCH:
                        part1(tc)
                    if tc >= 2:
                        if tc >= NCH:
                            deferred.append(lambda c=tc - 2: part2(c))
                        else:
                            part2(tc - 2, mid=lambda tc=tc: part1b(tc))
                            part1c(tc)
                    elif tc < NCH:
                        part1b(tc)
                        part1c(tc)
            add_block([(wv[:, kh * 4:(kh + 1) * 4, D + cb * 512:D + (cb + 1) * 512], (4, 512))
                       for cb in range(2) for kh in range(2)], vstage)

            for cp in range(4):
                def compute(views, cp=cp):
                    blk, bkey = views[0]
                    for tt in range(NT):
                        for ci in range(2):
                            cc = cp * 2 + ci
                            if deferred and tt == NT - 1:
                                for f in deferred:
                                    f()
                                del deferred[:]
                            b = 2 * ((NCH - 3) % 3) + cnt["g"] % 2
                            cnt["g"] += 1

                            def mm(e, ci=ci, tt=tt, b=b):
                                for kc in range(KC):
                                    i = e.matmul(ps[:, b, 0:TT], blk[:, kc, ci * 128:(ci + 1) * 128],
                                                 hn[:, kc, tl(tt)], start=(kc == 0), stop=(kc == KC - 1))
                                return i
                            P.add("pe", mm, reads=[bkey] + [("hn", kc, tt) for kc in range(KC)],
                                  writes=[("ps", b)])
                            P.add("dve", lambda e, cc=cc, tt=tt, b=b: e.tensor_tensor(
                                out=act[:, cc, tl(tt)], in0=ps[:, b, 0:TT], in1=act[:, cc, tl(tt)],
                                op=ALU.mult),
                                reads=[("ps", b), ("act", cc, tt)], writes=[("act", cc, tt)])
                add_block([(wv[:, :, cp * 256:(cp + 1) * 256], (KC, 256))], compute)
            proj_residual(sg_w_out[j], KC, after)

        def ffn(L, after):
            wg = wview(ffn_w_gate[L])
            wu = wview(ffn_w_up[L])
            for (f0, f1) in ((0, 8), (8, 16), (16, 22)):
                fc = f0
                while fc < f1:
                    nb = min(2, f1 - fc)

                    def compute(views, fc=fc, nb=nb, f0=f0):
                        (blkG, kG), (blkU, kU) = views
                        for fi in range(nb):
                            for tt in range(NT):
                                g = cnt["g"] % 2
                                cnt["g"] += 1
                                bG, bU = g, 2 + g

                                def mm(e, blk, b, fi=fi, tt=tt):
                                    for kc in range(KC):
                                        i = e.matmul(ps[:, b, 0:TT], blk[:, kc, fi * 128:(fi + 1) * 128],
                                                     hn[:, kc, tl(tt)], start=(kc == 0), stop=(kc == KC - 1))
                                    return i
                                hreads = [("hn", kc, tt) for kc in range(KC)]
                                P.add("pe", lambda e, blk=blkG, b=bG, mm=mm: mm(e, blk, b),
                                      reads=[kG] + hreads, writes=[("ps", bG)])
                                P.add("pe", lambda e, blk=blkU, b=bU, mm=mm: mm(e, blk, b),
                                      reads=[kU] + hreads, writes=[("ps", bU)])
                                ti = cnt["tmp"] % 2
                                cnt["tmp"] += 1
                                P.add("act", lambda e, b=bG, ti=ti: e.activation(
                                    out=tmpA[ti][:], in_=ps[:, b, 0:TT], func=AF.Silu),
                                    reads=[("ps", bG)], writes=[("tmpA", ti)])
                                a = fc + fi - f0
                                P.add("dve", lambda e, b=bU, ti=ti, a=a, tt=tt: e.tensor_tensor(
                                    out=act[:, a, tl(tt)], in0=ps[:, b, 0:TT], in1=tmpA[ti][:], op=ALU.mult),
                                    reads=[("ps", bU), ("tmpA", ti)], writes=[("act", a, tt)])
                    add_block([(wg[:, :, fc * 128:(fc + nb) * 128], (KC, nb * 128)),
                               (wu[:, :, fc * 128:(fc + nb) * 128], (KC, nb * 128))], compute)
                    fc += nb
                if f1 == FC:
                    ple_p_load(L)
                proj_residual(ffn_w_down[L][f0 * 128:f1 * 128, :], f1 - f0, after if f1 == FC else None)

        def ple(L, after):
            pv = pT[L].rearrange("(kc p) t -> p kc t", p=128)
            wpv = ple_w_proj[L].rearrange("(kc p) n -> p kc n", p=128)
            wgv = wview(ple_w_gate[L])
            shared = {}

            def tile_compute(tt, pblk, kp):
                gate = shared["gate"]
                wpp, kpp = shared["wpp"]
                for dc in range(KC):
                    blk, bkey = gate[dc // 2]
                    di = dc % 2
                    g = cnt["g"] % 3
                    cnt["g"] += 1
                    bG, bP = g, 3 + g

                    def mm(e, blk=blk, di=di, b=bG):
                        for kc in range(KC):
                            i = e.matmul(ps[:, b, 0:TT], blk[:, kc, di * 128:(di + 1) * 128],
                                         hn[:, kc, tl(tt)], start=(kc == 0), stop=(kc == KC - 1))
                        return i
                    P.add("pe", mm, reads=[bkey] + [("hn", kc, tt) for kc in range(KC)], writes=[("ps", bG)])

                    def mmp(e, dc=dc, b=bP):
                        for k2 in range(2):
                            i = e.matmul(ps[:, b, 0:TT], wpp[:, k2, dc * 128:(dc + 1) * 128],
                                         pblk[:, k2, :], start=(k2 == 0), stop=(k2 == 1))
                        return i
                    P.add("pe", mmp, reads=[kpp, kp] + [("act", 6, tt), ("act", 7, tt)], writes=[("ps", bP)])
                    ti = cnt["tmp"] % 2
                    cnt["tmp"] += 1
                    P.add("act", lambda e, b=bG, ti=ti: e.activation(
                        out=tmpA[ti][:], in_=ps[:, b, 0:TT], func=AF.Sigmoid),
                        reads=[("ps", bG)], writes=[("tmpA", ti)])
                    P.add("dve", lambda e, b=bP, ti=ti: e.tensor_tensor(
                        out=ps[:, b, 0:TT], in0=ps[:, b, 0:TT], in1=tmpA[ti][:], op=ALU.mult),
                        reads=[("ps", bP), ("tmpA", ti)], writes=[("ps", bP)])
                    P.add("dve", lambda e, dc=dc, b=bP: e.tensor_tensor(
                        out=h[:, dc, tl(tt)], in0=ps[:, b, 0:TT], in1=h[:, dc, tl(tt)], op=ALU.add),
                        reads=[("ps", bP), ("h", dc, tt)], writes=[("h", dc, tt)])
                    interleave(after, tt, dc)

            def main(views):
                shared["gate"] = views[0:4]
                shared["wpp"] = views[4]
                tile_compute(0, act[:, 6:8, tl(0)], "p")
            add_block([(wgv[:, :, dp * 256:(dp + 1) * 256], (KC, 256)) for dp in range(4)] +
                      [(wpv, (2, D))], main, hold=NT - 1)
            for tt in range(1, NT):
                add_block([], lambda views, tt=tt: tile_compute(tt, act[:, 6:8, tl(tt)], "p"))

        def ple_p_load(L):
            pv = pT[L].rearrange("(kc p) t -> p kc t", p=128)

            def compute(views):
                for k2 in range(2):
                    load_to(pv[:, k2:k2 + 1, :].rearrange("p a (c t) -> p (a c) t", t=min(T, STG_EL)),
                            (T // min(T, STG_EL), min(T, STG_EL)), act[:, 6 + k2, :],
                            ["p"] + [("act", 6 + k2, tt) for tt in range(NT)])
            add_block([], compute)

        norm_block(Norm(C_MIX + layers[0] * 8))
        if layers[0] % 2 == 1:
            sg_setup_early(layers[0] // 2)
        for li, L in enumerate(layers):
            j = L // 2
            n_ffn = Norm(C_FFN + L * 8)
            n_ple = Norm(C_PLE + L * 8)
            if li + 1 < len(layers):
                n_next = Norm(C_MIX + layers[li + 1] * 8)
            elif final:
                n_next = Norm(C_FIN, to_out=True)
            else:
                n_next = None
            if L % 2 == 0:
                conv_mixer(j, n_ffn)
            else:
                sg_mixer(j, n_ffn)
            ffn(L, n_ple)
            if li + 1 < len(layers) and layers[li + 1] % 2 == 1:
                sg_setup_early(layers[li + 1] // 2)
            ple(L, n_next)
        if not final:
            def dump(views):
                for kc in range(KC):
                    P.add("sp", lambda e, kc=kc: e.dma_start(out=outT[kc * 128:(kc + 1) * 128, :], in_=h[:, kc, :]),
                          reads=[("h", kc, tt) for tt in range(NT)], writes=["out"], dma_key="outh%d" % kc)
            add_block([], dump)

        from collections import deque
        free = deque(range(NSLOT))
        release_at = {}
        nblk = len(blocks)
        flat = [(bi, li) for bi in range(nblk) for li in range(len(blocks[bi]["loads"]))]
        for b in blocks:
            b["views"] = [None] * len(b["loads"])
            b["slots"] = []
        nxt = 0

        def issue_one():
            nonlocal nxt
            bi, li = flat[nxt]
            b = blocks[bi]
            si = free.popleft()
            ap, shp = b["loads"][li]
            b["views"][li] = load_block(ap, shp, si)
            release_at.setdefault(bi + b["hold"], []).append(si)
            nxt += 1

        assert not blocks[0]["loads"]
        if flat:
            first_blk = flat[0][0]
            while nxt < len(flat) and flat[nxt][0] == first_blk:
                issue_one()
        for tt in range(1, NT):
            load_x(tt)
        blocks[0]["compute"]([])
        for i in range(1, nblk):
            while nxt < len(flat) and flat[nxt][0] <= i:
                assert free, "not enough weight slots"
                issue_one()
            blocks[i]["compute"](blocks[i]["views"])
            for s in release_at.pop(i, []):
                free.append(s)
            while nxt < len(flat) and free:
                issue_one()

        P.add("sp", lambda e: e.nop(), reads=["out"] + [("out", kc, tt) for kc in range(KC) for tt in range(NT)],
              writes=["done"])

        with nc.Block() as block:
            P.finalize(block)
        P.close()
        PROG_STATS["ops"] = (len(P.ops), dict(P.eng_count), dict(P.sig_counts))
    return nc


def fm_cols(v):
    v = np.asarray(v, dtype=np.float32).reshape(-1, KC, 128)
    return np.ascontiguousarray(v.transpose(2, 0, 1).reshape(128, -1))


def make_in_maps(inp, T=SEQ, n_cores=N_CORES):
    f = lambda a: np.ascontiguousarray(np.asarray(a, dtype=np.float32))
    smalls = np.concatenate([
        fm_cols(inp["mix_norm"]), fm_cols(inp["ffn_norm"]), fm_cols(inp["ple_norm"]),
        fm_cols(inp["final_norm"]), fm_cols(np.asarray(inp["conv_w"]).reshape(-1, D)),
        fm_cols(inp["sg_v_bias"]), fm_cols(inp["sg_v_gain"])], axis=1)
    assert smalls.shape == (128, NS), smalls.shape
    shared = {
        "smalls": f(smalls),
        "conv_w_in": f(inp["conv_w_in"]), "conv_w_out": f(inp["conv_w_out"]),
        "sg_w_in": f(inp["sg_w_in"]), "sg_w_out": f(inp["sg_w_out"]),
        "sg_bs": f(np.asarray(inp["sg_b_spatial"]).reshape(2, D)),
        "sg_wsT": f(np.asarray(inp["sg_w_spatial"]).transpose(0, 3, 1, 2).reshape(2, 128, D)),
        "ffn_w_gate": f(inp["ffn_w_gate"]), "ffn_w_up": f(inp["ffn_w_up"]),
        "ffn_w_down": f(inp["ffn_w_down"]),
        "ple_w_gate": f(inp["ple_w_gate"]), "ple_w_proj": f(inp["ple_w_proj"]),
    }
    x = np.asarray(inp["x"], dtype=np.float32)
    p = np.asarray(inp["p"], dtype=np.float32)
    maps = []
    for c in range(n_cores):
        m = dict(shared)
        m["xT"] = np.ascontiguousarray(x[c].T)
        m["pT"] = np.ascontiguousarray(p[:, c].transpose(0, 2, 1))
        maps.append(m)
    return maps


_NC_CACHE = {}


def kernel(**inputs):
    x = np.asarray(inputs["x"])
    B, T, _ = x.shape
    assert B == N_CORES and T == SEQ
    if "nc" not in _NC_CACHE:
        _NC_CACHE["nc"] = build_program(T)
    nc = _NC_CACHE["nc"]
    in_maps = make_in_maps(inputs, T, N_CORES)
    res = run_bass_kernel_spmd(nc, in_maps, core_ids=list(range(N_CORES)))
    out = np.stack([np.asarray(r["outT"]).T for r in res.results], axis=0)
    return np.ascontiguousarray(out.astype(np.float32))
```
